# Optimizing a Trainium2 kernel written in Bass

```python
import jax, jax.numpy as jnp
from jax import lax
import numpy as np

D_MODEL = 4096
BATCH = 2
SEQ = 8192
DEPTH = 1

CHUNK = 64
PLE_DIM = 256
D_RNN = 2048
RNN_HEADS = 16
RNN_HEAD_DIM = D_RNN // RNN_HEADS
CONV_WIDTH = 4
LRU_C = 8.0
D_POOL = 2048
POOL_WINDOWS = (2, 4, 8, 16)
POOL_MAX = max(POOL_WINDOWS)
POOL_GROUPS = len(POOL_WINDOWS)
POOL_GROUP = D_POOL // POOL_GROUPS
D_MIX = D_RNN + D_POOL
D_IN = 2 * D_RNN + D_POOL
D_FF = 4 * D_MODEL
EPS = 1e-6

kernel_name = "hybrid_rglru_multiscale_pool_block"


def rmsnorm(x, g):
    xf = x.astype(jnp.float32)
    y = xf * lax.rsqrt(jnp.mean(xf * xf, axis=-1, keepdims=True) + EPS)
    return (y * g.astype(jnp.float32)).astype(x.dtype)


def causal_depthwise_conv(x, w, b):
    y = lax.conv_general_dilated(
        x, w[:, None, :], window_strides=(1,), padding=[(CONV_WIDTH - 1, 0)],
        dimension_numbers=("NWC", "WIO", "NWC"), feature_group_count=x.shape[-1])
    return y + b


def block_diag_linear(x, w, b):
    bsz, s, _ = x.shape
    xh = x.reshape(bsz, s, RNN_HEADS, RNN_HEAD_DIM)
    y = jnp.einsum("bshi,hij->bshj", xh, w).reshape(bsz, s, D_RNN)
    return y + b


def _lin_combine(left, right):
    a_l, b_l = left
    a_r, b_r = right
    return a_l * a_r, a_r * b_l + b_r


def chunked_linear_scan(a, b):
    bsz, s, c = a.shape
    n_chunks = s // CHUNK
    a_c = a.reshape(bsz, n_chunks, CHUNK, c).transpose(1, 0, 2, 3)
    b_c = b.reshape(bsz, n_chunks, CHUNK, c).transpose(1, 0, 2, 3)

    def step(h_prev, ab):
        ac, bc = ab
        a_cum, h_loc = lax.associative_scan(_lin_combine, (ac, bc), axis=1)
        h = h_loc + a_cum * h_prev[:, None, :]
        return h[:, -1], h

    _, hs = lax.scan(step, jnp.zeros((bsz, c), jnp.float32), (a_c, b_c))
    return hs.transpose(1, 0, 2, 3).reshape(bsz, s, c)


def rg_lru(x, w_a, b_a, w_x, b_x, lam):
    r = jax.nn.sigmoid(block_diag_linear(x, w_a, b_a).astype(jnp.float32))
    i = jax.nn.sigmoid(block_diag_linear(x, w_x, b_x).astype(jnp.float32))
    log_a = -LRU_C * r * jax.nn.softplus(-lam.astype(jnp.float32))
    a = jnp.exp(log_a)
    mult = jnp.sqrt(-jnp.expm1(2.0 * log_a))
    h = chunked_linear_scan(a, mult * i * x.astype(jnp.float32))
    return h.astype(x.dtype)


def multiscale_pool(v, w_pool, b_pool):
    bsz, s, _ = v.shape
    vf = v.astype(jnp.float32)
    cs = jnp.pad(jnp.cumsum(vf, axis=1), ((0, 0), (POOL_MAX, 0), (0, 0)))
    pos = jnp.arange(1, s + 1, dtype=jnp.int32)
    outs = []
    for g, w in enumerate(POOL_WINDOWS):
        lo, hi = g * POOL_GROUP, (g + 1) * POOL_GROUP
        win_sum = cs[:, POOL_MAX:POOL_MAX + s, lo:hi] - cs[:, POOL_MAX - w:POOL_MAX - w + s, lo:hi]
        cnt = jnp.minimum(pos, w).astype(jnp.float32)[None, :, None]
        outs.append(win_sum / cnt)
    z = (jnp.concatenate(outs, axis=-1) - vf).astype(v.dtype)
    z = z.reshape(bsz, s, POOL_GROUPS, POOL_GROUP)
    y = jnp.einsum("bsgi,gij->bsgj", z, w_pool).reshape(bsz, s, D_POOL)
    return y + b_pool


def setup_inputs(seed: int = 0) -> dict:
    key = jax.random.key(seed)
    ks = jax.random.split(key, 24)
    f32 = jnp.float32

    def nrm(k, shape, fan_in):
        return jax.random.normal(k, shape, f32) * (fan_in ** -0.5)

    def gain(k, shape):
        return 1.0 + 0.01 * jax.random.normal(k, shape, f32)

    def bias(k, shape):
        return 0.01 * jax.random.normal(k, shape, f32)

    u = jax.random.uniform(ks[10], (DEPTH, D_RNN), f32, minval=0.9, maxval=0.999)
    a0 = u ** (1.0 / LRU_C)
    lru_lambda = jnp.log(a0) - jnp.log1p(-a0)
    return {
        "x": jax.random.normal(ks[0], (BATCH, SEQ, D_MODEL), f32),
        "p": jax.random.normal(ks[1], (DEPTH, BATCH, SEQ, PLE_DIM), f32),
        "norm_mix_g": gain(ks[2], (DEPTH, D_MODEL)),
        "w_in": nrm(ks[3], (DEPTH, D_MODEL, D_IN), D_MODEL),
        "conv_w": nrm(ks[4], (DEPTH, CONV_WIDTH, D_RNN), CONV_WIDTH),
        "conv_b": bias(ks[5], (DEPTH, D_RNN)),
        "w_rg_a": nrm(ks[6], (DEPTH, RNN_HEADS, RNN_HEAD_DIM, RNN_HEAD_DIM), RNN_HEAD_DIM),
        "b_rg_a": bias(ks[7], (DEPTH, D_RNN)),
        "w_rg_x": nrm(ks[8], (DEPTH, RNN_HEADS, RNN_HEAD_DIM, RNN_HEAD_DIM), RNN_HEAD_DIM),
        "b_rg_x": bias(ks[9], (DEPTH, D_RNN)),
        "lru_lambda": lru_lambda,
        "beta_rnn": gain(ks[11], (DEPTH, D_RNN)),
        "w_pool": nrm(ks[12], (DEPTH, POOL_GROUPS, POOL_GROUP, POOL_GROUP), POOL_GROUP),
        "b_pool": bias(ks[13], (DEPTH, D_POOL)),
        "pool_scale": gain(ks[14], (DEPTH, D_POOL)),
        "w_out": nrm(ks[15], (DEPTH, D_MIX, D_MODEL), D_MIX),
        "norm_mlp_g": gain(ks[16], (DEPTH, D_MODEL)),
        "w_up": nrm(ks[17], (DEPTH, D_MODEL, D_FF), D_MODEL),
        "w_down": nrm(ks[18], (DEPTH, D_FF, D_MODEL), D_FF),
        "norm_ple_g": gain(ks[19], (DEPTH, D_MODEL)),
        "w_ple_gate": nrm(ks[20], (DEPTH, D_MODEL, D_MODEL), D_MODEL),
        "w_ple_proj": nrm(ks[21], (DEPTH, PLE_DIM, D_MODEL), PLE_DIM),
        "final_norm_g": gain(ks[22], (D_MODEL,)),
    }


def reference(x, p, norm_mix_g, w_in, conv_w, conv_b, w_rg_a, b_rg_a, w_rg_x, b_rg_x,
              lru_lambda, beta_rnn, w_pool, b_pool, pool_scale, w_out, norm_mlp_g,
              w_up, w_down, norm_ple_g, w_ple_gate, w_ple_proj, final_norm_g):
    h = x
    for i in range(DEPTH):
        u = rmsnorm(h, norm_mix_g[i])
        proj = u @ w_in[i]
        xr = proj[..., :D_RNN]
        gr = proj[..., D_RNN:2 * D_RNN]
        v = proj[..., 2 * D_RNN:]
        xr = causal_depthwise_conv(xr, conv_w[i], conv_b[i])
        y_rnn = rg_lru(xr, w_rg_a[i], b_rg_a[i], w_rg_x[i], b_rg_x[i], lru_lambda[i])
        y_rnn = rmsnorm(y_rnn * jax.nn.gelu(gr, approximate=True), beta_rnn[i])
        y_pool = rmsnorm(multiscale_pool(v, w_pool[i], b_pool[i]), pool_scale[i])
        h = h + jnp.concatenate([y_rnn, y_pool], axis=-1) @ w_out[i]
        u = rmsnorm(h, norm_mlp_g[i])
        h = h + jnp.square(jax.nn.relu(u @ w_up[i])) @ w_down[i]
        gate = jax.nn.sigmoid(rmsnorm(h, norm_ple_g[i]) @ w_ple_gate[i])
        h = h + gate * (p[i] @ w_ple_proj[i])
    return rmsnorm(h, final_norm_g)
```

```python
import numpy as np
import concourse.bass as bass
import concourse.mybir as mybir
from concourse.bass_utils import run_bass_kernel_spmd
from contextlib import ExitStack

F32 = mybir.dt.float32
F32R = mybir.dt.float32r
AF = mybir.ActivationFunctionType
ALU = mybir.AluOpType

D = 4096
NK = D // 128
DR = 2048
DP = 2048
PLE = 256
T = 512
NCORE = 8
EPS = 1e-6
LRU_C = 8.0

C_GMIX = 0
C_GMLP = 32
C_GPLE = 64
C_GFIN = 96
C_CONVW = 128
C_CONVB = 192
C_BA = 208
C_BX = 224
C_LAM = 240
C_BETA = 256
C_BPOOL = 272
C_PSCALE = 288
C_MASK = 304
C_INVCNT = 307
C_EPS = 371
C_ONE = 372
C_NEG = 373
NPRM = 374


class Sem:
    def __init__(self, h, name):
        self.h = h
        self.name = name
        self.count = 0


class Res:
    __slots__ = ("name", "w", "r")

    def __init__(self, name):
        self.name = name
        self.w = None
        self.r = {}


class Stream:
    def __init__(self, name, sem):
        self.name = name
        self.sem = sem
        self.ops = []
        self.seen = {}


class Rec:
    def __init__(self):
        self.streams = {}

    def add_stream(self, name, sem):
        self.streams[name] = Stream(name, sem)

    def _waits(self, st, reads, writes):
        need = {}

        def add(ev, ordered_ok):
            if ev is None:
                return
            sem, val = ev
            if ordered_ok and sem is st.sem:
                return
            if need.get(sem, 0) < val:
                need[sem] = val

        for r in reads:
            add(r.w, False)
        for w in writes:
            add(w.w, True)
            for sem, val in w.r.items():
                add((sem, val), True)
        for sem, val in need.items():
            if st.seen.get(sem, 0) < val:
                st.seen[sem] = val
                st.ops.append(("wait", sem, val))

    def _mark(self, ev, reads, writes):
        sem, val = ev
        for r in reads:
            if r.r.get(sem, 0) < val:
                r.r[sem] = val
        for w in writes:
            w.w = ev
            w.r = {}

    def op(self, stname, fn, reads=(), writes=(), inc=True):
        st = self.streams[stname]
        self._waits(st, reads, writes)
        if inc:
            st.sem.count += 1
            ev = (st.sem, st.sem.count)
            st.ops.append(("opi", fn))
        else:
            ev = (st.sem, st.sem.count + 1)
            st.ops.append(("op", fn))
        self._mark(ev, reads, writes)

    def dma(self, out_ap, in_ap, reads, writes, sem, cast=False, queue="sp"):
        st = self.streams[queue]
        self._waits(st, reads, writes)
        sem.count += 16
        ev = (sem, sem.count)
        st.ops.append(("dma", out_ap, in_ap, sem, cast))
        self._mark(ev, reads, writes)

    def final_wait(self, stname, sems):
        st = self.streams[stname]
        for s in sems:
            if s.count > 0:
                st.ops.append(("wait", s, s.count))

    def replay(self, nc, eng, stname):
        st = self.streams[stname]
        for o in st.ops:
            k = o[0]
            if k == "wait":
                eng.wait_ge(o[1].h, o[2])
            elif k == "opi":
                o[1](eng).then_inc(st.sem.h, 1)
            elif k == "op":
                o[1](eng)
            else:
                _, out_ap, in_ap, sem, cast = o
                nc.dge_precook = not cast
                eng.dma_start(out=out_ap, in_=in_ap).then_inc(sem.h, 16)
        nc.dge_precook = True


def alias_barrier(new_res, old_res):
    acc = {}
    for o in old_res:
        if o.w is not None:
            s, v = o.w
            if acc.get(s, 0) < v:
                acc[s] = v
        for s, v in o.r.items():
            if acc.get(s, 0) < v:
                acc[s] = v
    for n in new_res:
        n.w = None
        n.r = dict(acc)


def build_nc(NT, DFF, NPRIOR=3, debug_taps=False):
    NF = DFF // 128
    NGRP = NF // 8
    SEG = NT * T
    nc = bass.Bass("TRN2", target_bir_lowering=False)

    def dram(name, shape, kind="ExternalInput"):
        return nc.dram_tensor(name, shape, F32, kind=kind).ap()

    xm_d = dram("xm", [D, SEG])
    xp_d = dram("xp", [NPRIOR * D, SEG])
    pt_d = dram("pt", [PLE, SEG])
    prm_d = dram("prm", [128, NPRM])
    win_d = dram("win", [48 * 2 * 128, 2048])
    wg_d = dram("wg", [16 * 128, 256])
    wpool_d = dram("wpool", [4 * 128, 2048])
    wout_d = dram("wout", [32 * 2 * 128, 2048])
    wup_d = dram("wup", [NF * 2 * 128, 2048])
    wdn_d = dram("wdn", [NGRP * 16 * 128, 2048])
    wgate_d = dram("wgate", [32 * 2 * 128, 2048])
    wproj_d = dram("wproj", [32 * 128, 256])
    out_d = dram("out", [D, SEG], kind="ExternalOutput")
    TAPS = ("u1", "mix", "h1", "h2", "h3", "xc", "a", "b", "hs") if debug_taps else ()
    tap_d = {n: dram("tap_" + n, [D, SEG], kind="ExternalOutput") for n in TAPS}

    es = ExitStack()
    with es:
        def sb(name, cols, dt=F32):
            return es.enter_context(nc.sbuf_tensor(name, [128, cols], dt))

        def arena(name, cols):
            off = (nc.sbuf_base + 31) // 32 * 32
            t = sb(name, cols)
            tr = nc.alloc_sbuf_tensor_at(name + "r", [128, cols], F32R, offset=off)
            return t, tr

        R1, R1R = arena("R1", NK * T)
        R2, R2R = arena("R2", NK * T)
        R3, R3R = arena("R3", 8192)
        WS = [sb(f"W{i}", 2048, F32R) for i in range(4)]
        WSM = [sb(f"WSM{i}", 256, F32R) for i in range(4)]
        ONES = sb("ONES", 128, F32R)
        PRM = sb("PRM", NPRM)
        DER = sb("DER", 80)
        DTMP = sb("DTMP", 64)
        RSTD = sb("RSTD", T)
        HALO = sb("HALO", 16 * 3)
        VHALO = sb("VHALO", 16 * 16)
        STATE = sb("STATE", 16)
        TMP16 = sb("TMP16", 16)
        SQ = [sb("SQ0", T, F32R), sb("SQ1", T, F32R)]
        PS = [es.enter_context(nc.psum_tensor(f"ps{i}", [128, T], F32)) for i in range(8)]

        def sem(name):
            return Sem(es.enter_context(nc.semaphore(name)), name)

        rec = Rec()
        for n in ("pe", "act", "dve", "pool"):
            rec.add_stream(n, sem("s_" + n))
        rec.add_stream("sp", sem("s_sp"))
        w_sem = [sem(f"s_w{i}") for i in range(4)]
        wsm_sem = [sem(f"s_wsm{i}") for i in range(4)]
        r2_sem = [sem(f"s_r2_{i}") for i in range(NK)]
        r1_sem = [sem(f"s_r1_{i}") for i in range(NK)]
        pt_sem = sem("s_pt")
        prm_sem = sem("s_prm")

        r1 = [Res(f"r1_{k}") for k in range(NK)]
        r2 = [Res(f"r2_{k}") for k in range(NK)]
        w_res = [Res(f"w{i}") for i in range(4)]
        wsm_res = [Res(f"wsm{i}") for i in range(4)]
        ps_res = [Res(f"ps{i}") for i in range(8)]
        ones_r, prm_r, der_r, dtmp_r = Res("ones"), Res("prm"), Res("der"), Res("dtmp")
        rstd_r, negh_r, half_r = Res("rstd"), Res("negh"), Res("half")
        halo_r = [Res(f"halo{h}") for h in range(16)]
        vhalo_r = [Res(f"vhalo{h}") for h in range(16)]
        state_r = [Res(f"state{h}") for h in range(16)]
        tmp16_r, sq_r = Res("tmp16"), [Res("sq0"), Res("sq1")]

        def r3(off, n):
            return R3[:, off:off + n]

        def r3r(off, n):
            return R3R[:, off:off + n]
        XR = r3(0, 528)
        XC, TA, A2, TX, BB, HS, GQ = [r3(528 + i * T, T) for i in range(7)]
        o = 528 + 7 * T
        V, SA, SBt = r3(o, 528), r3(o + 528, 528), r3(o + 1056, 528)
        ZG = [r3(o + 1584 + i * T, T) for i in range(4)]
        ZGr = [r3r(o + 1584 + i * T, T) for i in range(4)]
        XCr = r3r(528, T)
        mix_names = ["XR", "XC", "TA", "A2", "TX", "BB", "HS", "GQ", "V", "SA", "SB", "ZG0", "ZG1", "ZG2", "ZG3"]
        HB = [[r3((b * 8 + j) * T, T) for j in range(8)] for b in range(2)]
        HBr = [[r3r((b * 8 + j) * T, T) for j in range(8)] for b in range(2)]
        PTr = [r3r(kp * T, T) for kp in range(2)]
        TG, TG2 = r3(2 * T, T), r3(3 * T, T)

        class V3:
            pass
        v3 = V3()
        v3.mix = {n: Res("m_" + n) for n in mix_names}
        v3.hb = [[Res(f"hb{b}_{j}") for j in range(8)] for b in range(2)]
        v3.ple = {n: Res("p_" + n) for n in ("PT", "TG", "TG2")}
        v3.cur = "mix"

        def r3_all(view):
            if view == "mix":
                return list(v3.mix.values())
            if view == "hb":
                return [x for row in v3.hb for x in row]
            return list(v3.ple.values())

        def r3_switch(view):
            if v3.cur != view:
                alias_barrier(r3_all(view), r3_all(v3.cur))
                v3.cur = view

        def R1c(k):
            return R1[:, k * T:(k + 1) * T]

        def R2c(k):
            return R2[:, k * T:(k + 1) * T]

        def R1r(k):
            return R1R[:, k * T:(k + 1) * T]

        def R2r(k):
            return R2R[:, k * T:(k + 1) * T]

        def pcol(c):
            return PRM[:, c:c + 1]

        def dcol(c):
            return DER[:, c:c + 1]

        tap_sem = sem("s_tap")

        def tap(name, chunk_ap, chunk_res, col0):
            if name not in tap_d:
                return
            for k in range(NK):
                rec.dma(tap_d[name][k * 128:(k + 1) * 128, col0:col0 + T], chunk_ap(k), reads=[chunk_res[k]], writes=[],
                        sem=tap_sem)

        def tap1(name, row0, ap, res, col0):
            if name in tap_d:
                rec.dma(tap_d[name][row0:row0 + 128, col0:col0 + T], ap, reads=[res], writes=[], sem=tap_sem)

        wstate = {"i": 0}

        def wload(dram_rows, ncols):
            s = wstate["i"] % 4
            wstate["i"] += 1
            rec.dma(WS[s][:, 0:ncols], dram_rows.bitcast(F32R), reads=[], writes=[w_res[s]], sem=w_sem[s], cast=True)
            return WS[s], w_res[s]

        def wload_small(dram_rows):
            s_ = wstate.get("j", 0) % 4
            wstate["j"] = wstate.get("j", 0) + 1
            rec.dma(WSM[s_][:, :], dram_rows.bitcast(F32R), reads=[], writes=[wsm_res[s_]], sem=wsm_sem[s_], cast=True)
            return WSM[s_], wsm_res[s_]

        pstate = {"i": 0, "s": 0, "q": 0}

        def next_bank():
            b = pstate["i"] % 6
            pstate["i"] += 1
            return b

        def next_stat_bank():
            b = 6 + pstate["s"] % 2
            pstate["s"] += 1
            return b

        def mm_group(bank, pairs, reads, ncol=T):
            n = len(pairs)

            def fn(e, pairs=pairs, bank=bank, n=n, ncol=ncol):
                ins = None
                for i, (l, r) in enumerate(pairs):
                    ins = e.matmul(PS[bank][:, 0:ncol], l, r, start=(i == 0), stop=(i == n - 1))
                return ins
            rec.op("pe", fn, reads=reads, writes=[ps_res[bank]])

        class Stat:
            def __init__(self, n):
                self.n = n
                self.i = 0
                self.bank = next_stat_bank()
                self.pending = False

            def square(self, src_ap, src_res):
                p = pstate.get("pend")
                if p is not None and p is not self:
                    p.flush()
                pstate["pend"] = self
                self.flush()
                b = pstate["q"] % 2
                pstate["q"] += 1
                self.buf = b
                rec.op("act", lambda e, s=src_ap, b=b: e.activation(out=SQ[b][:], in_=s, func=AF.Square),
                       reads=src_res, writes=[sq_r[b]])
                self.pending = True

            def flush(self):
                if self.pending:
                    self.pending = False
                    self.mm()

            def mm(self):
                i, n, bank, b = self.i, self.n, self.bank, self.buf
                self.i += 1
                rec.op("pe", lambda e: e.matmul(PS[bank][:], ONES[:], SQ[b][:], start=(i == 0), stop=(i == n - 1)),
                       reads=[sq_r[b], ones_r], writes=[ps_res[bank]])

            def finish(self, dim):
                self.flush()
                assert self.i == self.n
                bank = self.bank
                rec.op("act", lambda e: e.activation(out=RSTD[:], in_=PS[bank][:], func=AF.Ln, scale=1.0 / dim, bias=pcol(C_EPS)),
                       reads=[ps_res[bank], prm_r], writes=[rstd_r])
                rec.op("act", lambda e: e.activation(out=RSTD[:], in_=RSTD[:], func=AF.Exp, scale=-0.5),
                       reads=[rstd_r], writes=[rstd_r])

        def scale_to(dst_ap, dst_res, src_ap, src_res, gcol):
            d = dst_ap
            rec.op("dve", lambda e: e.scalar_tensor_tensor(out=d, in0=src_ap, scalar=pcol(gcol), in1=RSTD[:],
                                                           op0=ALU.mult, op1=ALU.mult),
                   reads=[src_res, rstd_r, prm_r], writes=[dst_res])

        rec.dma(PRM[:], prm_d[:, :], reads=[], writes=[prm_r], sem=prm_sem)
        rec.op("dve", lambda e: e.tensor_scalar(out=ONES[:], in0=PRM[:, 0:128], scalar1=0.0, scalar2=1.0,
                                                op0=ALU.mult, op1=ALU.add), reads=[prm_r], writes=[ones_r])
        rec.op("dve", lambda e: e.memset(HALO[:], 0.0), writes=halo_r)
        rec.op("dve", lambda e: e.memset(VHALO[:], 0.0), writes=vhalo_r)
        rec.op("dve", lambda e: e.memset(STATE[:], 0.0), writes=state_r)
        E_, U_, L_, D_ = DTMP[:, 0:16], DTMP[:, 16:32], DTMP[:, 32:48], DTMP[:, 48:64]
        rec.op("act", lambda e: e.activation(out=E_, in_=PRM[:, C_LAM:C_LAM + 16], func=AF.Exp, scale=-1.0),
               reads=[prm_r], writes=[dtmp_r])
        rec.op("dve", lambda e: e.tensor_scalar(out=U_, in0=E_, scalar1=1.0, scalar2=None, op0=ALU.add),
               reads=[dtmp_r], writes=[dtmp_r])
        rec.op("act", lambda e: e.activation(out=L_, in_=U_, func=AF.Ln), reads=[dtmp_r], writes=[dtmp_r])
        rec.op("dve", lambda e: e.tensor_scalar(out=D_, in0=U_, scalar1=-1.0, scalar2=1e-30, op0=ALU.add, op1=ALU.max),
               reads=[dtmp_r], writes=[dtmp_r])
        rec.op("dve", lambda e: e.reciprocal(out=D_, in_=D_), reads=[dtmp_r], writes=[dtmp_r])
        rec.op("dve", lambda e: e.tensor_tensor(out=L_, in0=L_, in1=E_, op=ALU.mult), reads=[dtmp_r], writes=[dtmp_r])
        rec.op("dve", lambda e: e.tensor_tensor(out=L_, in0=L_, in1=D_, op=ALU.mult), reads=[dtmp_r], writes=[dtmp_r])
        rec.op("dve", lambda e: e.tensor_scalar(out=DER[:, 0:16], in0=L_, scalar1=-0.5 * LRU_C, scalar2=None, op0=ALU.mult),
               reads=[dtmp_r], writes=[der_r])
        rec.op("dve", lambda e: e.tensor_scalar(out=DER[:, 16:32], in0=L_, scalar1=-LRU_C, scalar2=None, op0=ALU.mult),
               reads=[dtmp_r], writes=[der_r])
        rec.op("dve", lambda e: e.tensor_scalar(out=DER[:, 64:80], in0=L_, scalar1=-2.0 * LRU_C, scalar2=None, op0=ALU.mult),
               reads=[dtmp_r], writes=[der_r])
        rec.op("dve", lambda e: e.tensor_scalar(out=DER[:, 32:48], in0=PRM[:, C_BA:C_BA + 16], scalar1=-1.0, scalar2=None,
                                                op0=ALU.mult), reads=[prm_r], writes=[der_r])
        rec.op("dve", lambda e: e.tensor_scalar(out=DER[:, 48:64], in0=PRM[:, C_BX:C_BX + 16], scalar1=-1.0, scalar2=None,
                                                op0=ALU.mult), reads=[prm_r], writes=[der_r])

        class Reg:
            pass
        RA, RB = Reg(), Reg()
        RA.c, RA.r, RA.res, RA.sem = R2c, R2r, r2, r2_sem
        RB.c, RB.r, RB.res, RB.sem = R1c, R1r, r1, r1_sem

        def load_x(reg, src_d, row0, col0, queue="sp", ks=None):
            for k in (range(NK) if ks is None else ks):
                rec.dma(reg.c(k), src_d[row0 + k * 128: row0 + (k + 1) * 128, col0:col0 + T],
                        reads=[], writes=[reg.res[k]], sem=reg.sem[k], queue=queue)

        class NormJob:
            def __init__(self, reg, gcol0):
                self.reg, self.gcol0 = reg, gcol0
                self.st = None
                self.ksq = 0
                self.ksc = 0

            def squares(self, n):
                for _ in range(n):
                    if self.ksq < NK:
                        k = self.ksq
                        p = pstate.get("pend")
                        if p is not None:
                            p.flush()
                            pstate["pend"] = None
                        b = pstate["q"] % 2
                        pstate["q"] += 1
                        rec.op("act", lambda e, k=k, b=b: e.activation(out=SQ[b][:], in_=self.reg.c(k), func=AF.Square),
                               reads=[self.reg.res[k]], writes=[sq_r[b]])
                        if k == 0:
                            rec.op("dve", lambda e, b=b: e.tensor_copy(out=RSTD[:], in_=SQ[b][:].bitcast(F32)),
                                   reads=[sq_r[b]], writes=[rstd_r])
                        else:
                            rec.op("dve", lambda e, b=b: e.tensor_tensor(out=RSTD[:], in0=RSTD[:], in1=SQ[b][:].bitcast(F32),
                                                                         op=ALU.add),
                                   reads=[sq_r[b], rstd_r], writes=[rstd_r])
                        self.ksq += 1

            def finish(self):
                assert self.ksq == NK
                bank = next_stat_bank()
                rec.op("dve", lambda e: e.tensor_copy(out=SQ[0][:], in_=RSTD[:]), reads=[rstd_r], writes=[sq_r[0]])
                rec.op("pe", lambda e: e.matmul(PS[bank][:], ONES[:], SQ[0][:], start=True, stop=True),
                       reads=[sq_r[0], ones_r], writes=[ps_res[bank]])
                rec.op("act", lambda e: e.activation(out=RSTD[:], in_=PS[bank][:], func=AF.Ln, scale=1.0 / D, bias=pcol(C_EPS)),
                       reads=[ps_res[bank], prm_r], writes=[rstd_r])
                rec.op("act", lambda e: e.activation(out=RSTD[:], in_=RSTD[:], func=AF.Exp, scale=-0.5),
                       reads=[rstd_r], writes=[rstd_r])

            def scales(self, n):
                for _ in range(n):
                    if self.ksc < NK:
                        k = self.ksc
                        scale_to(self.reg.r(k), self.reg.res[k], self.reg.c(k), self.reg.res[k], self.gcol0 + k)
                        self.ksc += 1

            def all(self):
                self.squares(NK)
                self.finish()
                self.scales(NK)

        def proj_group(wd, idx, rhs, rhs_res, bank, ncol=T):
            for hf in range(2):
                row = (idx * 2 + hf) * 128
                wt, wr = wload(wd[row:row + 128, :], 2048)
                pairs = [(wt[:, kc * 128:(kc + 1) * 128], rhs(hf * 16 + kc)) for kc in range(16)]

                def fn(e, pairs=pairs, hf=hf, bank=bank, ncol=ncol):
                    ins = None
                    for i, (l, r) in enumerate(pairs):
                        ins = e.matmul(PS[bank][:, 0:ncol], l, r, start=(hf == 0 and i == 0), stop=(hf == 1 and i == 15))
                    return ins
                rec.op("pe", fn, reads=[wr] + list(rhs_res[hf * 16:(hf + 1) * 16]), writes=[ps_res[bank]])

        def win_panel_group(c, reg, bank, ncol=T, rhs=None):
            proj_group(win_d, c, rhs if rhs is not None else reg.r, reg.res, bank, ncol=ncol)

        ALT = [(XR, XC, XCr, "XR", "XC"), (V, ZG[0], ZGr[0], "V", "ZG0"), (SA, ZG[1], ZGr[1], "SA", "ZG1")]

        def rnn_A(h, mix=v3.mix, alt=0, reg=None):
            XR_, XC_, XCr_, nXR, nXC = ALT[int(alt)]
            return _rnn_A(h, mix, XR_, XC_, XCr_, nXR, nXC, reg if reg is not None else RA)

        def _rnn_A(h, mix, XR, XC, XCr, nXR, nXC, reg):
            bank = next_bank()
            win_panel_group(h, reg, bank)
            rec.op("pool", lambda e: e.tensor_copy(out=XR[:, 0:3], in_=HALO[:, h * 3:h * 3 + 3]),
                   reads=[halo_r[h]], writes=[mix[nXR]])
            rec.op("act", lambda e: e.activation(out=XR[:, 3:515], in_=PS[bank][:], func=AF.Copy),
                   reads=[ps_res[bank]], writes=[mix[nXR]])
            rec.op("pool", lambda e: e.tensor_copy(out=HALO[:, h * 3:h * 3 + 3], in_=XR[:, 512:515]),
                   reads=[mix[nXR]], writes=[halo_r[h]])
            rec.op("dve", lambda e: e.tensor_scalar(out=XC, in0=XR[:, 0:512], scalar1=pcol(C_CONVW + 0 * 16 + h),
                                                    scalar2=pcol(C_CONVB + h), op0=ALU.mult, op1=ALU.add),
                   reads=[mix[nXR], prm_r], writes=[mix[nXC]])
            for k in (1, 2):
                rec.op("dve", lambda e, k=k: e.scalar_tensor_tensor(out=XC, in0=XR[:, k:k + 512],
                                                                    scalar=pcol(C_CONVW + k * 16 + h), in1=XC,
                                                                    op0=ALU.mult, op1=ALU.add),
                       reads=[mix[nXR], mix[nXC], prm_r], writes=[mix[nXC]])
            rec.op("dve", lambda e: e.scalar_tensor_tensor(out=XCr, in0=XR[:, 3:515],
                                                           scalar=pcol(C_CONVW + 3 * 16 + h), in1=XC,
                                                           op0=ALU.mult, op1=ALU.add),
                   reads=[mix[nXR], mix[nXC], prm_r], writes=[mix[nXC]])

        def rnn_B(h, mix=v3.mix, tapcol=None, alt=0):
            _, XC_, XCr_, _, nXC = ALT[int(alt)]
            return _rnn_B(h, mix, tapcol, XC_, XCr_, nXC)

        def _rnn_B(h, mix, tapcol, XC, XCr, nXC):
            wt, wr = wload_small(wg_d[h * 128:(h + 1) * 128, :])
            if tapcol is not None:
                tap1("xc", h * 128, XC, mix[nXC], tapcol)
            ba, bx = next_bank(), next_bank()
            mm_group(ba, [(wt[:, 0:128], XCr)], [wr, mix[nXC]])
            mm_group(bx, [(wt[:, 128:256], XCr)], [wr, mix[nXC]])
            rec.op("act", lambda e: e.activation(out=TA, in_=PS[ba][:], func=AF.Exp, bias=dcol(32 + h), scale=-1.0),
                   reads=[ps_res[ba], der_r], writes=[mix["TA"]])
            rec.op("act", lambda e: e.activation(out=TA, in_=TA, func=AF.Ln, bias=pcol(C_ONE), scale=1.0),
                   reads=[mix["TA"], prm_r], writes=[mix["TA"]])
            rec.op("act", lambda e: e.activation(out=TA, in_=TA, func=AF.Exp, scale=-1.0),
                   reads=[mix["TA"]], writes=[mix["TA"]])
            rec.op("act", lambda e: e.activation(out=A2, in_=TA, func=AF.Exp, bias=pcol(C_NEG), scale=dcol(48 + 16 + h)),
                   reads=[mix["TA"], der_r, prm_r], writes=[mix["A2"]])
            rec.op("act", lambda e: e.activation(out=TA, in_=TA, func=AF.Exp, scale=dcol(16 + h)),
                   reads=[mix["TA"], der_r], writes=[mix["TA"]])
            rec.op("act", lambda e: e.activation(out=A2, in_=A2, func=AF.Ln, bias=pcol(C_ONE), scale=-1.0),
                   reads=[mix["A2"], prm_r], writes=[mix["A2"]])
            rec.op("act", lambda e: e.activation(out=TX, in_=PS[bx][:], func=AF.Exp, bias=dcol(48 + h), scale=-1.0),
                   reads=[ps_res[bx], der_r], writes=[mix["TX"]])
            rec.op("act", lambda e: e.activation(out=TX, in_=TX, func=AF.Ln, bias=pcol(C_ONE), scale=1.0),
                   reads=[mix["TX"], prm_r], writes=[mix["TX"]])
            rec.op("dve", lambda e: e.scalar_tensor_tensor(out=BB, in0=A2, scalar=0.5, in1=TX, op0=ALU.mult, op1=ALU.subtract),
                   reads=[mix["A2"], mix["TX"]], writes=[mix["BB"]])
            rec.op("act", lambda e: e.activation(out=BB, in_=BB, func=AF.Exp),
                   reads=[mix["BB"]], writes=[mix["BB"]])
            rec.op("dve", lambda e: e.tensor_tensor(out=BB, in0=BB, in1=XC, op=ALU.mult),
                   reads=[mix["BB"], mix[nXC]], writes=[mix["BB"]])
            rec.op("dve", lambda e: e.tensor_tensor_scan(out=HS, data0=TA, data1=BB, initial=STATE[:, h:h + 1],
                                                         op0=ALU.mult, op1=ALU.add),
                   reads=[mix["TA"], mix["BB"], state_r[h]], writes=[mix["HS"]])
            rec.op("pool", lambda e: e.tensor_copy(out=STATE[:, h:h + 1], in_=HS[:, T - 1:T]),
                   reads=[mix["HS"]], writes=[state_r[h]])
            if tapcol is not None:
                tap1("a", h * 128, TA, mix["TA"], tapcol)
                tap1("b", h * 128, BB, mix["BB"], tapcol)
                tap1("hs", h * 128, HS, mix["HS"], tapcol)

        def rnn_C(h, statA, mix=v3.mix):
            bank = next_bank()
            win_panel_group(16 + h, RA, bank)
            P = PS[bank][:]
            rec.op("act", lambda e: e.activation(out=GQ, in_=P, func=AF.Square), reads=[ps_res[bank]], writes=[mix["GQ"]])
            rec.op("dve", lambda e: e.tensor_scalar(out=GQ, in0=GQ, scalar1=0.044715, scalar2=1.0, op0=ALU.mult, op1=ALU.add),
                   reads=[mix["GQ"]], writes=[mix["GQ"]])
            rec.op("dve", lambda e: e.tensor_tensor(out=GQ, in0=GQ, in1=P, op=ALU.mult),
                   reads=[mix["GQ"], ps_res[bank]], writes=[mix["GQ"]])
            rec.op("act", lambda e: e.activation(out=GQ, in_=GQ, func=AF.Exp, scale=-2.0 * 0.7978845608028654),
                   reads=[mix["GQ"]], writes=[mix["GQ"]])
            rec.op("act", lambda e: e.activation(out=GQ, in_=GQ, func=AF.Ln, bias=pcol(C_ONE), scale=1.0),
                   reads=[mix["GQ"], prm_r], writes=[mix["GQ"]])
            rec.op("act", lambda e: e.activation(out=GQ, in_=GQ, func=AF.Exp, scale=-1.0),
                   reads=[mix["GQ"]], writes=[mix["GQ"]])
            rec.op("dve", lambda e: e.tensor_tensor(out=GQ, in0=GQ, in1=P, op=ALU.mult),
                   reads=[mix["GQ"], ps_res[bank]], writes=[mix["GQ"]])
            return bank

        def rnn_C2(h, statA, mix=v3.mix):
            rec.op("dve", lambda e: e.tensor_tensor(out=R1c(h), in0=GQ, in1=HS, op=ALU.mult),
                   reads=[mix["GQ"], mix["HS"]], writes=[r1[h]])
            statA.square(R1c(h), [r1[h]])

        def pool_V(c, first_tile, mix=v3.mix):
            g = c // 4
            bank = next_bank()
            win_panel_group(32 + c, RA, bank)
            rec.op("pool", lambda e: e.tensor_copy(out=V[:, 0:16], in_=VHALO[:, c * 16:(c + 1) * 16]),
                   reads=[vhalo_r[c]], writes=[mix["V"]])
            rec.op("act", lambda e: e.activation(out=V[:, 16:528], in_=PS[bank][:], func=AF.Copy),
                   reads=[ps_res[bank]], writes=[mix["V"]])
            rec.op("pool", lambda e: e.tensor_copy(out=VHALO[:, c * 16:(c + 1) * 16], in_=V[:, 512:528]),
                   reads=[mix["V"]], writes=[vhalo_r[c]])
            srcs = [(V, "V"), (SA, "SA"), (SBt, "SB"), (SA, "SA"), (SBt, "SB")]
            cur, curn = V, "V"
            sh = 1
            for lvl in range(g + 1):
                dst, dstn = srcs[lvl + 1]
                lo = 2 * sh - 1
                rec.op("dve", lambda e, dst=dst, cur=cur, lo=lo, sh=sh: e.tensor_tensor(
                    out=dst[:, lo:528], in0=cur[:, lo:528], in1=cur[:, lo - sh:528 - sh], op=ALU.add),
                    reads=[mix[curn]], writes=[mix[dstn]])
                cur, curn = dst, dstn
                sh *= 2
            w = 2 ** (g + 1)
            zi = c % 4
            rec.op("dve", lambda e, cur=cur: e.scalar_tensor_tensor(out=ZGr[zi], in0=cur[:, 16:528], scalar=1.0 / w,
                                                                     in1=V[:, 16:528], op0=ALU.mult, op1=ALU.subtract),
                   reads=[mix[curn], mix["V"]], writes=[mix[f"ZG{zi}"]])
            if first_tile:
                rec.op("dve", lambda e, cur=cur: e.tensor_tensor(out=TMP16[:], in0=cur[:, 16:32],
                                                                  in1=PRM[:, C_INVCNT + g * 16:C_INVCNT + (g + 1) * 16],
                                                                  op=ALU.mult),
                       reads=[mix[curn], prm_r], writes=[tmp16_r])
                rec.op("dve", lambda e: e.tensor_tensor(out=ZGr[zi][:, 0:16], in0=TMP16[:], in1=V[:, 16:32],
                                                        op=ALU.subtract),
                       reads=[tmp16_r, mix["V"]], writes=[mix[f"ZG{zi}"]])

        def pool_LIN(g, statB, mix=v3.mix):
            wt, wr = wload(wpool_d[g * 128:(g + 1) * 128, :], 2048)
            for jo in range(4):
                bank = next_bank()
                pairs = [(wt[:, ki * 512 + jo * 128: ki * 512 + (jo + 1) * 128], ZGr[ki]) for ki in range(4)]
                mm_group(bank, pairs, [wr] + [mix[f"ZG{ki}"] for ki in range(4)])
                c = g * 4 + jo
                rec.op("act", lambda e, bank=bank, c=c: e.activation(out=R1c(16 + c), in_=PS[bank][:], func=AF.Identity,
                                                                      bias=pcol(C_BPOOL + c)),
                       reads=[ps_res[bank], prm_r], writes=[r1[16 + c]])
                statB.square(R1c(16 + c), [r1[16 + c]])

        def light_pass():
            tiles = [(slot, i) for slot in range(NPRIOR) for i in range(NT)]
            L = len(tiles)
            regs = [RA, RB]

            def src(n):
                return (xp_d, tiles[n][0] * D, tiles[n][1] * T) if n < L else (xm_d, 0, 0)
            r3_switch("mix")
            reg0 = regs[L % 2]
            load_x(reg0, *src(0))
            NormJob(reg0, C_GMIX).all()
            for n in range(L):
                slot, i = tiles[n]
                reg, nxt = regs[(L - n) % 2], regs[(L - n - 1) % 2]
                job = NormJob(nxt, C_GMIX)
                for h in range(16):
                    rnn_A(h, alt=h % 3, reg=reg)
                    if h >= 2:
                        rnn_B(h - 2, alt=(h - 2) % 3)
                    if h == 0:
                        load_x(nxt, *src(n + 1), queue="act")
                    if 2 <= h <= 9:
                        job.squares(4)
                    if h == 10:
                        job.finish()
                    if h >= 10:
                        job.scales(6 if h < 12 else 5)
                rnn_B(14, alt=14 % 3)
                rnn_B(15, alt=15 % 3)
                if slot == NPRIOR - 1 and i == NT - 1:
                    for c in range(16):
                        bank = next_bank()
                        win_panel_group(32 + c, reg, bank, ncol=16, rhs=lambda k, reg=reg: reg.r(k)[:, T - 16:T])
                        rec.op("dve", lambda e, bank=bank, c=c: e.tensor_scalar(
                            out=VHALO[:, c * 16:(c + 1) * 16], in0=PS[bank][:, 0:16], scalar1=pcol(C_MASK + slot),
                            scalar2=None, op0=ALU.mult), reads=[ps_res[bank], prm_r], writes=[vhalo_r[c]])
                if i == NT - 1:
                    rec.op("dve", lambda e, slot=slot: e.tensor_scalar(out=STATE[:], in0=STATE[:], scalar1=pcol(C_MASK + slot),
                                                                       scalar2=None, op0=ALU.mult),
                           reads=state_r + [prm_r], writes=state_r)
                    rec.op("dve", lambda e, slot=slot: e.tensor_scalar(out=HALO[:], in0=HALO[:], scalar1=pcol(C_MASK + slot),
                                                                       scalar2=None, op0=ALU.mult),
                           reads=halo_r + [prm_r], writes=halo_r)

        def main_tile(i, prefetched=False):
            col0 = i * T
            r3_switch("mix")
            if not prefetched:
                load_x(RA, xm_d, 0, col0)
                NormJob(RA, C_GMIX).all()
            tap("u1", R2c, r2, col0)
            statA = Stat(16)
            for h in range(16):
                rnn_A(h)
                rnn_C(h, statA)
                statA.flush()
                rnn_B(h, tapcol=col0)
                rnn_C2(h, statA)
            statB = Stat(16)
            for c in range(16):
                pool_V(c, first_tile=(i == 0))
                if c % 4 == 3:
                    pool_LIN(c // 4, statB)
            for k in range(NK):
                rec.dma(R2c(k), xm_d[k * 128:(k + 1) * 128, col0:col0 + T], reads=[], writes=[r2[k]], sem=r2_sem[k])
            statA.finish(DR)
            for h in range(16):
                scale_to(R1r(h), r1[h], R1c(h), r1[h], C_BETA + h)
            statB.finish(DP)
            for c in range(16):
                scale_to(R1r(16 + c), r1[16 + c], R1c(16 + c), r1[16 + c], C_PSCALE + c)

            tap("mix", R1c, r1, col0)
            def big_proj(wd, m, rhs, rhs_res):
                bank = next_bank()
                proj_group(wd, m, rhs, rhs_res, bank)
                return bank

            st2 = Stat(NK)
            for m in range(NK):
                bank = big_proj(wout_d, m, R1r, r1)
                rec.op("dve", lambda e, bank=bank, m=m: e.tensor_tensor(out=R2c(m), in0=PS[bank][:], in1=R2c(m), op=ALU.add),
                       reads=[ps_res[bank], r2[m]], writes=[r2[m]])
                st2.square(R2c(m), [r2[m]])
            st2.finish(D)
            tap("h1", R2c, r2, col0)
            for k in range(NK):
                scale_to(R1r(k), r1[k], R2c(k), r2[k], C_GMLP + k)

            r3_switch("hb")

            def mlp_up(grp):
                b = grp % 2
                for j in range(8):
                    f = grp * 8 + j
                    bank = big_proj(wup_d, f, R1r, r1)
                    rec.op("act", lambda e, bank=bank, b=b, j=j: e.activation(out=HB[b][j], in_=PS[bank][:], func=AF.Relu),
                           reads=[ps_res[bank]], writes=[v3.hb[b][j]])
                    rec.op("dve", lambda e, b=b, j=j: e.tensor_tensor(out=HBr[b][j], in0=HB[b][j], in1=HB[b][j], op=ALU.mult),
                           reads=[v3.hb[b][j]], writes=[v3.hb[b][j]])

            st3 = Stat(NK)

            def mlp_down(grp):
                b = grp % 2
                last = grp == NGRP - 1
                for mp in range(16):
                    row = (grp * 16 + mp) * 128
                    wt, wr = wload(wdn_d[row:row + 128, :], 2048)
                    for mm_ in range(2):
                        m = mp * 2 + mm_
                        bank = next_bank()
                        pairs = [(wt[:, (mm_ * 8 + kc) * 128:(mm_ * 8 + kc + 1) * 128], HBr[b][kc]) for kc in range(8)]
                        mm_group(bank, pairs, [wr] + v3.hb[b])
                        rec.op("dve", lambda e, bank=bank, m=m: e.tensor_tensor(out=R2c(m), in0=PS[bank][:], in1=R2c(m), op=ALU.add),
                               reads=[ps_res[bank], r2[m]], writes=[r2[m]])
                        if last:
                            st3.square(R2c(m), [r2[m]])

            mlp_up(0)
            for grp in range(NGRP):
                if grp + 1 < NGRP:
                    mlp_up(grp + 1)
                mlp_down(grp)
            st3.finish(D)
            tap("h2", R2c, r2, col0)
            for k in range(NK):
                scale_to(R1r(k), r1[k], R2c(k), r2[k], C_GPLE + k)

            r3_switch("ple")
            for kp in range(2):
                rec.dma(PTr[kp], pt_d[kp * 128:(kp + 1) * 128, col0:col0 + T].bitcast(F32R), reads=[],
                        writes=[v3.ple["PT"]], sem=pt_sem, cast=True)
            st4 = Stat(NK)
            for m in range(NK):
                bg = big_proj(wgate_d, m, R1r, r1)
                wt, wr = wload_small(wproj_d[m * 128:(m + 1) * 128, :])
                bp = next_bank()
                mm_group(bp, [(wt[:, kp * 128:(kp + 1) * 128], PTr[kp]) for kp in range(2)], [wr, v3.ple["PT"]])
                rec.op("act", lambda e, bg=bg: e.activation(out=TG, in_=PS[bg][:], func=AF.Exp, scale=-1.0),
                       reads=[ps_res[bg]], writes=[v3.ple["TG"]])
                rec.op("act", lambda e: e.activation(out=TG, in_=TG, func=AF.Ln, bias=pcol(C_ONE), scale=1.0),
                       reads=[v3.ple["TG"], prm_r], writes=[v3.ple["TG"]])
                rec.op("act", lambda e: e.activation(out=TG, in_=TG, func=AF.Exp, scale=-1.0),
                       reads=[v3.ple["TG"]], writes=[v3.ple["TG"]])
                rec.op("dve", lambda e, bp=bp: e.tensor_tensor(out=TG2, in0=TG, in1=PS[bp][:], op=ALU.mult),
                       reads=[v3.ple["TG"], ps_res[bp]], writes=[v3.ple["TG2"]])
                rec.op("dve", lambda e, m=m: e.tensor_tensor(out=R2c(m), in0=TG2, in1=R2c(m), op=ALU.add),
                       reads=[v3.ple["TG2"], r2[m]], writes=[r2[m]])
                st4.square(R2c(m), [r2[m]])
            st4.finish(D)
            tap("h3", R2c, r2, col0)
            for m in range(NK):
                scale_to(R2c(m), r2[m], R2c(m), r2[m], C_GFIN + m)
                rec.dma(out_d[m * 128:(m + 1) * 128, col0:col0 + T], R2c(m), reads=[r2[m]], writes=[], sem=r2_sem[m])

        light_pass()
        for i in range(NT):
            main_tile(i, prefetched=(i == 0))
        rec.final_wait("sp", r2_sem + r1_sem + [tap_sem])

        with nc.Block() as block:
            @block.tensor
            def _(e):
                rec.replay(nc, e, "pe")

            @block.scalar
            def _(e):
                rec.replay(nc, e, "act")

            @block.vector
            def _(e):
                rec.replay(nc, e, "dve")

            @block.gpsimd
            def _(e):
                rec.replay(nc, e, "pool")

            @block.sync
            def _(e):
                rec.replay(nc, e, "sp")
    return nc


def _chunked(v, n):
    return np.ascontiguousarray(np.asarray(v, np.float32).reshape(n, 128).T)


def _panel_k2(w):
    K, M = w.shape
    assert K == 4096
    a = w.reshape(2, 16, 128, M // 128, 128)
    a = a.transpose(3, 0, 2, 1, 4)
    return np.ascontiguousarray(a).reshape(-1, 2048)


def prep_weights(w_in, w_rg_a, w_rg_x, w_pool, w_out, w_up, w_down, w_ple_gate, w_ple_proj):
    DFF = w_up.shape[1]
    ngrp = DFF // 1024
    d = {}
    d["win"] = _panel_k2(w_in)
    wg = np.concatenate([w_rg_a, w_rg_x], axis=2)
    d["wg"] = np.ascontiguousarray(wg).reshape(16 * 128, 256)
    wp = w_pool.reshape(4, 4, 128, 512).transpose(0, 2, 1, 3)
    d["wpool"] = np.ascontiguousarray(wp).reshape(4 * 128, 2048)
    d["wout"] = _panel_k2(w_out)
    d["wup"] = _panel_k2(w_up)
    a = w_down.reshape(ngrp, 8, 128, 16, 2, 128)
    a = a.transpose(0, 3, 2, 4, 1, 5)
    d["wdn"] = np.ascontiguousarray(a).reshape(ngrp * 16 * 128, 2048)
    d["wgate"] = _panel_k2(w_ple_gate)
    a = w_ple_proj.reshape(2, 128, 32, 128).transpose(2, 1, 0, 3)
    d["wproj"] = np.ascontiguousarray(a).reshape(32 * 128, 256)
    return d


def prep_prm(q, norm_mix_g, norm_mlp_g, norm_ple_g, final_norm_g, conv_w, conv_b, b_rg_a, b_rg_x, lru_lambda,
             beta_rnn, b_pool, pool_scale, nprior=3):
    prm = np.zeros((128, NPRM), np.float32)
    prm[:, C_GMIX:C_GMIX + 32] = _chunked(norm_mix_g, 32)
    prm[:, C_GMLP:C_GMLP + 32] = _chunked(norm_mlp_g, 32)
    prm[:, C_GPLE:C_GPLE + 32] = _chunked(norm_ple_g, 32)
    prm[:, C_GFIN:C_GFIN + 32] = _chunked(final_norm_g, 32)
    for k in range(4):
        prm[:, C_CONVW + k * 16:C_CONVW + (k + 1) * 16] = _chunked(conv_w[k], 16)
    prm[:, C_CONVB:C_CONVB + 16] = _chunked(conv_b, 16)
    prm[:, C_BA:C_BA + 16] = _chunked(b_rg_a, 16)
    prm[:, C_BX:C_BX + 16] = _chunked(b_rg_x, 16)
    prm[:, C_LAM:C_LAM + 16] = _chunked(lru_lambda, 16)
    prm[:, C_BETA:C_BETA + 16] = _chunked(beta_rnn, 16)
    prm[:, C_BPOOL:C_BPOOL + 16] = _chunked(b_pool, 16)
    prm[:, C_PSCALE:C_PSCALE + 16] = _chunked(pool_scale, 16)
    prm[:, C_EPS] = EPS
    prm[:, C_ONE] = 1.0
    prm[:, C_NEG] = -1e-7
    for j in range(nprior):
        prm[:, C_MASK + j] = 1.0 if (q - nprior + j) >= 0 else 0.0
    for g in range(4):
        w = 2 ** (g + 1)
        for t in range(16):
            cnt = min(t + 1, w) if q == 0 else w
            prm[:, C_INVCNT + g * 16 + t] = np.float32(1.0) / np.float32(cnt)
    return prm


def run(inputs, NT, trace=False, debug_taps=False):
    x = np.asarray(inputs["x"], np.float32)
    p = np.asarray(inputs["p"], np.float32)[0]
    B, S, _ = x.shape
    QPB = NCORE // B
    SEG = S // QPB
    assert SEG == NT * T
    NPRIOR = QPB - 1
    DFF = inputs["w_up"].shape[2]
    wd = prep_weights(inputs["w_in"][0], inputs["w_rg_a"][0], inputs["w_rg_x"][0], inputs["w_pool"][0],
                      inputs["w_out"][0], inputs["w_up"][0], inputs["w_down"][0], inputs["w_ple_gate"][0],
                      inputs["w_ple_proj"][0])
    wd = {k: np.asarray(v, np.float32) for k, v in wd.items()}
    nc = build_nc(NT, DFF, NPRIOR, debug_taps=debug_taps)
    in_maps = []
    xT = [np.ascontiguousarray(x[b].T) for b in range(B)]
    pT = [np.ascontiguousarray(p[b].T) for b in range(B)]
    for c in range(NCORE):
        b, q = divmod(c, QPB)
        xm = np.ascontiguousarray(xT[b][:, q * SEG:(q + 1) * SEG])
        xp = np.zeros((NPRIOR * D, SEG), np.float32)
        for j in range(NPRIOR):
            sidx = q - NPRIOR + j
            if sidx >= 0:
                xp[j * D:(j + 1) * D] = xT[b][:, sidx * SEG:(sidx + 1) * SEG]
        prm = prep_prm(q, inputs["norm_mix_g"][0], inputs["norm_mlp_g"][0], inputs["norm_ple_g"][0],
                       inputs["final_norm_g"], inputs["conv_w"][0], inputs["conv_b"][0], inputs["b_rg_a"][0],
                       inputs["b_rg_x"][0], inputs["lru_lambda"][0], inputs["beta_rnn"][0], inputs["b_pool"][0],
                       inputs["pool_scale"][0], nprior=NPRIOR)
        m = {"xm": xm, "xp": xp, "pt": np.ascontiguousarray(pT[b][:, q * SEG:(q + 1) * SEG]), "prm": prm}
        m.update(wd)
        in_maps.append(m)
    res = run_bass_kernel_spmd(nc, in_maps, core_ids=list(range(NCORE)), trace=trace)
    out = np.empty((B, S, D), np.float32)
    for c in range(NCORE):
        b, q = divmod(c, QPB)
        out[b, q * SEG:(q + 1) * SEG, :] = res.results[c]["out"].T
    if debug_taps:
        taps = {}
        for n in ("u1", "mix", "h1", "h2", "h3", "xc", "a", "b", "hs"):
            a = np.empty((B, S, D), np.float32)
            for c in range(NCORE):
                b, q = divmod(c, QPB)
                a[b, q * SEG:(q + 1) * SEG, :] = res.results[c]["tap_" + n].T
            taps[n] = a
        return out, res, taps
    return out, res


def kernel(**inputs):
    out, _ = run(inputs, NT=4)
    return out
```

```python
import numpy as np
import concourse.bass as bass
import concourse.mybir as mybir
from concourse.bass_utils import run_bass_kernel_spmd
from contextlib import ExitStack

F32 = mybir.dt.float32
F32R = mybir.dt.float32r
AF = mybir.ActivationFunctionType
ALU = mybir.AluOpType

D = 4096
NK = D // 128
DR = 2048
DP = 2048
PLE = 256
T = 512
NCORE = 8
EPS = 1e-6
LRU_C = 8.0

C_GMIX = 0
C_GMLP = 32
C_GPLE = 64
C_GFIN = 96
C_CONVW = 128
C_CONVB = 192
C_BA = 208
C_BX = 224
C_LAM = 240
C_BETA = 256
C_BPOOL = 272
C_PSCALE = 288
C_MASK = 304
C_INVCNT = 307
C_EPS = 371
C_ONE = 372
C_NEG = 373
NPRM = 374


class Sem:
    def __init__(self, h, name):
        self.h = h
        self.name = name
        self.count = 0


class Res:
    __slots__ = ("name", "w", "r")

    def __init__(self, name):
        self.name = name
        self.w = None
        self.r = {}


class Stream:
    def __init__(self, name, sem):
        self.name = name
        self.sem = sem
        self.ops = []
        self.seen = {}


class Rec:
    def __init__(self):
        self.streams = {}

    def add_stream(self, name, sem):
        self.streams[name] = Stream(name, sem)

    def _waits(self, st, reads, writes):
        need = {}

        def add(ev, ordered_ok):
            if ev is None:
                return
            sem, val = ev
            if ordered_ok and sem is st.sem:
                return
            if need.get(sem, 0) < val:
                need[sem] = val

        for r in reads:
            add(r.w, False)
        for w in writes:
            add(w.w, True)
            for sem, val in w.r.items():
                add((sem, val), True)
        for sem, val in need.items():
            if st.seen.get(sem, 0) < val:
                st.seen[sem] = val
                st.ops.append(("wait", sem, val))

    def _mark(self, ev, reads, writes):
        sem, val = ev
        for r in reads:
            if r.r.get(sem, 0) < val:
                r.r[sem] = val
        for w in writes:
            w.w = ev
            w.r = {}

    def op(self, stname, fn, reads=(), writes=(), inc=True):
        st = self.streams[stname]
        self._waits(st, reads, writes)
        if inc:
            st.sem.count += 1
            ev = (st.sem, st.sem.count)
            st.ops.append(("opi", fn))
        else:
            ev = (st.sem, st.sem.count + 1)
            st.ops.append(("op", fn))
        self._mark(ev, reads, writes)

    def dma(self, out_ap, in_ap, reads, writes, sem, cast=False, queue="sp"):
        st = self.streams[queue]
        self._waits(st, reads, writes)
        sem.count += 16
        ev = (sem, sem.count)
        st.ops.append(("dma", out_ap, in_ap, sem, cast))
        self._mark(ev, reads, writes)

    def final_wait(self, stname, sems):
        st = self.streams[stname]
        for s in sems:
            if s.count > 0:
                st.ops.append(("wait", s, s.count))

    def replay(self, nc, eng, stname):
        st = self.streams[stname]
        for o in st.ops:
            k = o[0]
            if k == "wait":
                eng.wait_ge(o[1].h, o[2])
            elif k == "opi":
                o[1](eng).then_inc(st.sem.h, 1)
            elif k == "op":
                o[1](eng)
            else:
                _, out_ap, in_ap, sem, cast = o
                nc.dge_precook = not cast
                eng.dma_start(out=out_ap, in_=in_ap).then_inc(sem.h, 16)
        nc.dge_precook = True


def alias_barrier(new_res, old_res):
    acc = {}
    for o in old_res:
        if o.w is not None:
            s, v = o.w
            if acc.get(s, 0) < v:
                acc[s] = v
        for s, v in o.r.items():
            if acc.get(s, 0) < v:
                acc[s] = v
    for n in new_res:
        n.w = None
        n.r = dict(acc)


def build_nc(NT, DFF, NPRIOR=3, debug_taps=False):
    NF = DFF // 128
    NGRP = NF // 8
    SEG = NT * T
    nc = bass.Bass("TRN2", target_bir_lowering=False)

    def dram(name, shape, kind="ExternalInput"):
        return nc.dram_tensor(name, shape, F32, kind=kind).ap()

    xm_d = dram("xm", [D, SEG])
    xp_d = dram("xp", [NPRIOR * D, SEG])
    pt_d = dram("pt", [PLE, SEG])
    prm_d = dram("prm", [128, NPRM])
    win_d = dram("win", [48 * 2 * 128, 2048])
    wg_d = dram("wg", [16 * 128, 256])
    wpool_d = dram("wpool", [4 * 128, 2048])
    wout_d = dram("wout", [32 * 2 * 128, 2048])
    wup_d = dram("wup", [NF * 2 * 128, 2048])
    wdn_d = dram("wdn", [NGRP * 16 * 128, 2048])
    wgate_d = dram("wgate", [32 * 2 * 128, 2048])
    wproj_d = dram("wproj", [32 * 128, 256])
    out_d = dram("out", [D, SEG], kind="ExternalOutput")
    TAPS = ("u1", "mix", "h1", "h2", "h3", "xc", "a", "b", "hs") if debug_taps else ()
    tap_d = {n: dram("tap_" + n, [D, SEG], kind="ExternalOutput") for n in TAPS}

    es = ExitStack()
    with es:
        def sb(name, cols, dt=F32):
            return es.enter_context(nc.sbuf_tensor(name, [128, cols], dt))

        def arena(name, cols):
            off = (nc.sbuf_base + 31) // 32 * 32
            t = sb(name, cols)
            tr = nc.alloc_sbuf_tensor_at(name + "r", [128, cols], F32R, offset=off)
            return t, tr

        R1, R1R = arena("R1", NK * T)
        R2, R2R = arena("R2", NK * T)
        R3, R3R = arena("R3", 8192)
        WS = [sb(f"W{i}", 2048, F32R) for i in range(4)]
        WSM = [sb(f"WSM{i}", 256, F32R) for i in range(4)]
        ONES = sb("ONES", 128, F32R)
        PRM = sb("PRM", NPRM)
        DER = sb("DER", 80)
        DTMP = sb("DTMP", 64)
        RSTD = sb("RSTD", T)
        HALO = sb("HALO", 16 * 3)
        VHALO = sb("VHALO", 16 * 16)
        STATE = sb("STATE", 16)
        TMP16 = sb("TMP16", 16)
        SQ = [sb("SQ0", T, F32R), sb("SQ1", T, F32R)]
        PS = [es.enter_context(nc.psum_tensor(f"ps{i}", [128, T], F32)) for i in range(8)]

        def sem(name):
            return Sem(es.enter_context(nc.semaphore(name)), name)

        rec = Rec()
        for n in ("pe", "act", "dve", "pool"):
            rec.add_stream(n, sem("s_" + n))
        rec.add_stream("sp", sem("s_sp"))
        w_sem = [sem(f"s_w{i}") for i in range(4)]
        wsm_sem = [sem(f"s_wsm{i}") for i in range(4)]
        r2_sem = [sem(f"s_r2_{i}") for i in range(NK)]
        r1_sem = [sem(f"s_r1_{i}") for i in range(NK)]
        pt_sem = sem("s_pt")
        prm_sem = sem("s_prm")

        r1 = [Res(f"r1_{k}") for k in range(NK)]
        r2 = [Res(f"r2_{k}") for k in range(NK)]
        w_res = [Res(f"w{i}") for i in range(4)]
        wsm_res = [Res(f"wsm{i}") for i in range(4)]
        ps_res = [Res(f"ps{i}") for i in range(8)]
        ones_r, prm_r, der_r, dtmp_r = Res("ones"), Res("prm"), Res("der"), Res("dtmp")
        rstd_r, negh_r, half_r = Res("rstd"), Res("negh"), Res("half")
        halo_r = [Res(f"halo{h}") for h in range(16)]
        vhalo_r = [Res(f"vhalo{h}") for h in range(16)]
        state_r = [Res(f"state{h}") for h in range(16)]
        tmp16_r, sq_r = Res("tmp16"), [Res("sq0"), Res("sq1")]

        def r3(off, n):
            return R3[:, off:off + n]

        def r3r(off, n):
            return R3R[:, off:off + n]
        XR = r3(0, 528)
        XC, TA, A2, TX, BB, HS, GQ = [r3(528 + i * T, T) for i in range(7)]
        o = 528 + 7 * T
        V, SA, SBt = r3(o, 528), r3(o + 528, 528), r3(o + 1056, 528)
        ZG = [r3(o + 1584 + i * T, T) for i in range(4)]
        ZGr = [r3r(o + 1584 + i * T, T) for i in range(4)]
        XCr = r3r(528, T)
        mix_names = ["XR", "XC", "TA", "A2", "TX", "BB", "HS", "GQ", "V", "SA", "SB", "ZG0", "ZG1", "ZG2", "ZG3"]
        HB = [[r3((b * 8 + j) * T, T) for j in range(8)] for b in range(2)]
        HBr = [[r3r((b * 8 + j) * T, T) for j in range(8)] for b in range(2)]
        PTr = [r3r(kp * T, T) for kp in range(2)]
        TG, TG2 = r3(2 * T, T), r3(3 * T, T)

        class V3:
            pass
        v3 = V3()
        v3.mix = {n: Res("m_" + n) for n in mix_names}
        v3.hb = [[Res(f"hb{b}_{j}") for j in range(8)] for b in range(2)]
        v3.ple = {n: Res("p_" + n) for n in ("PT", "TG", "TG2")}
        v3.cur = "mix"

        def r3_all(view):
            if view == "mix":
                return list(v3.mix.values())
            if view == "hb":
                return [x for row in v3.hb for x in row]
            return list(v3.ple.values())

        def r3_switch(view):
            if v3.cur != view:
                alias_barrier(r3_all(view), r3_all(v3.cur))
                v3.cur = view

        def R1c(k):
            return R1[:, k * T:(k + 1) * T]

        def R2c(k):
            return R2[:, k * T:(k + 1) * T]

        def R1r(k):
            return R1R[:, k * T:(k + 1) * T]

        def R2r(k):
            return R2R[:, k * T:(k + 1) * T]

        def pcol(c):
            return PRM[:, c:c + 1]

        def dcol(c):
            return DER[:, c:c + 1]

        tap_sem = sem("s_tap")

        def tap(name, chunk_ap, chunk_res, col0):
            if name not in tap_d:
                return
            for k in range(NK):
                rec.dma(tap_d[name][k * 128:(k + 1) * 128, col0:col0 + T], chunk_ap(k), reads=[chunk_res[k]], writes=[],
                        sem=tap_sem)

        def tap1(name, row0, ap, res, col0):
            if name in tap_d:
                rec.dma(tap_d[name][row0:row0 + 128, col0:col0 + T], ap, reads=[res], writes=[], sem=tap_sem)

        wstate = {"i": 0}

        def wload(dram_rows, ncols):
            s = wstate["i"] % 4
            wstate["i"] += 1
            rec.dma(WS[s][:, 0:ncols], dram_rows.bitcast(F32R), reads=[], writes=[w_res[s]], sem=w_sem[s], cast=True)
            return WS[s], w_res[s]

        def wload_small(dram_rows):
            s_ = wstate.get("j", 0) % 4
            wstate["j"] = wstate.get("j", 0) + 1
            rec.dma(WSM[s_][:, :], dram_rows.bitcast(F32R), reads=[], writes=[wsm_res[s_]], sem=wsm_sem[s_], cast=True)
            return WSM[s_], wsm_res[s_]

        pstate = {"i": 0, "s": 0, "q": 0}

        def next_bank():
            b = pstate["i"] % 6
            pstate["i"] += 1
            return b

        def next_stat_bank():
            b = 6 + pstate["s"] % 2
            pstate["s"] += 1
            return b

        def mm_group(bank, pairs, reads, ncol=T):
            n = len(pairs)

            def fn(e, pairs=pairs, bank=bank, n=n, ncol=ncol):
                ins = None
                for i, (l, r) in enumerate(pairs):
                    ins = e.matmul(PS[bank][:, 0:ncol], l, r, start=(i == 0), stop=(i == n - 1))
                return ins
            rec.op("pe", fn, reads=reads, writes=[ps_res[bank]])

        class Stat:
            def __init__(self, n):
                self.n = n
                self.i = 0
                self.bank = next_stat_bank()
                self.pending = False

            def square(self, src_ap, src_res):
                p = pstate.get("pend")
                if p is not None and p is not self:
                    p.flush()
                pstate["pend"] = self
                self.flush()
                b = pstate["q"] % 2
                pstate["q"] += 1
                self.buf = b
                rec.op("act", lambda e, s=src_ap, b=b: e.activation(out=SQ[b][:], in_=s, func=AF.Square),
                       reads=src_res, writes=[sq_r[b]])
                self.pending = True

            def flush(self):
                if self.pending:
                    self.pending = False
                    self.mm()

            def mm(self):
                i, n, bank, b = self.i, self.n, self.bank, self.buf
                self.i += 1
                rec.op("pe", lambda e: e.matmul(PS[bank][:], ONES[:], SQ[b][:], start=(i == 0), stop=(i == n - 1)),
                       reads=[sq_r[b], ones_r], writes=[ps_res[bank]])

            def finish(self, dim):
                self.flush()
                assert self.i == self.n
                bank = self.bank
                rec.op("act", lambda e: e.activation(out=RSTD[:], in_=PS[bank][:], func=AF.Ln, scale=1.0 / dim, bias=pcol(C_EPS)),
                       reads=[ps_res[bank], prm_r], writes=[rstd_r])
                rec.op("act", lambda e: e.activation(out=RSTD[:], in_=RSTD[:], func=AF.Exp, scale=-0.5),
                       reads=[rstd_r], writes=[rstd_r])

        def scale_to(dst_ap, dst_res, src_ap, src_res, gcol):
            d = dst_ap
            rec.op("dve", lambda e: e.scalar_tensor_tensor(out=d, in0=src_ap, scalar=pcol(gcol), in1=RSTD[:],
                                                           op0=ALU.mult, op1=ALU.mult),
                   reads=[src_res, rstd_r, prm_r], writes=[dst_res])

        rec.dma(PRM[:], prm_d[:, :], reads=[], writes=[prm_r], sem=prm_sem)
        rec.op("dve", lambda e: e.tensor_scalar(out=ONES[:], in0=PRM[:, 0:128], scalar1=0.0, scalar2=1.0,
                                                op0=ALU.mult, op1=ALU.add), reads=[prm_r], writes=[ones_r])
        rec.op("dve", lambda e: e.memset(HALO[:], 0.0), writes=halo_r)
        rec.op("dve", lambda e: e.memset(VHALO[:], 0.0), writes=vhalo_r)
        rec.op("dve", lambda e: e.memset(STATE[:], 0.0), writes=state_r)
        E_, U_, L_, D_ = DTMP[:, 0:16], DTMP[:, 16:32], DTMP[:, 32:48], DTMP[:, 48:64]
        rec.op("act", lambda e: e.activation(out=E_, in_=PRM[:, C_LAM:C_LAM + 16], func=AF.Exp, scale=-1.0),
               reads=[prm_r], writes=[dtmp_r])
        rec.op("dve", lambda e: e.tensor_scalar(out=U_, in0=E_, scalar1=1.0, scalar2=None, op0=ALU.add),
               reads=[dtmp_r], writes=[dtmp_r])
        rec.op("act", lambda e: e.activation(out=L_, in_=U_, func=AF.Ln), reads=[dtmp_r], writes=[dtmp_r])
        rec.op("dve", lambda e: e.tensor_scalar(out=D_, in0=U_, scalar1=-1.0, scalar2=1e-30, op0=ALU.add, op1=ALU.max),
               reads=[dtmp_r], writes=[dtmp_r])
        rec.op("dve", lambda e: e.reciprocal(out=D_, in_=D_), reads=[dtmp_r], writes=[dtmp_r])
        rec.op("dve", lambda e: e.tensor_tensor(out=L_, in0=L_, in1=E_, op=ALU.mult), reads=[dtmp_r], writes=[dtmp_r])
        rec.op("dve", lambda e: e.tensor_tensor(out=L_, in0=L_, in1=D_, op=ALU.mult), reads=[dtmp_r], writes=[dtmp_r])
        rec.op("dve", lambda e: e.tensor_scalar(out=DER[:, 0:16], in0=L_, scalar1=-0.5 * LRU_C, scalar2=None, op0=ALU.mult),
               reads=[dtmp_r], writes=[der_r])
        rec.op("dve", lambda e: e.tensor_scalar(out=DER[:, 16:32], in0=L_, scalar1=-LRU_C, scalar2=None, op0=ALU.mult),
               reads=[dtmp_r], writes=[der_r])
        rec.op("dve", lambda e: e.tensor_scalar(out=DER[:, 64:80], in0=L_, scalar1=-2.0 * LRU_C, scalar2=None, op0=ALU.mult),
               reads=[dtmp_r], writes=[der_r])
        rec.op("dve", lambda e: e.tensor_scalar(out=DER[:, 32:48], in0=PRM[:, C_BA:C_BA + 16], scalar1=-1.0, scalar2=None,
                                                op0=ALU.mult), reads=[prm_r], writes=[der_r])
        rec.op("dve", lambda e: e.tensor_scalar(out=DER[:, 48:64], in0=PRM[:, C_BX:C_BX + 16], scalar1=-1.0, scalar2=None,
                                                op0=ALU.mult), reads=[prm_r], writes=[der_r])

        class Reg:
            pass
        RA, RB = Reg(), Reg()
        RA.c, RA.r, RA.res, RA.sem = R2c, R2r, r2, r2_sem
        RB.c, RB.r, RB.res, RB.sem = R1c, R1r, r1, r1_sem

        def load_x(reg, src_d, row0, col0, queue="sp", ks=None):
            for k in (range(NK) if ks is None else ks):
                rec.dma(reg.c(k), src_d[row0 + k * 128: row0 + (k + 1) * 128, col0:col0 + T],
                        reads=[], writes=[reg.res[k]], sem=reg.sem[k], queue=queue)

        class NormJob:
            def __init__(self, reg, gcol0):
                self.reg, self.gcol0 = reg, gcol0
                self.st = None
                self.ksq = 0
                self.ksc = 0

            def squares(self, n):
                for _ in range(n):
                    if self.ksq < NK:
                        k = self.ksq
                        p = pstate.get("pend")
                        if p is not None:
                            p.flush()
                            pstate["pend"] = None
                        b = pstate["q"] % 2
                        pstate["q"] += 1
                        rec.op("act", lambda e, k=k, b=b: e.activation(out=SQ[b][:], in_=self.reg.c(k), func=AF.Square),
                               reads=[self.reg.res[k]], writes=[sq_r[b]])
                        if k == 0:
                            rec.op("dve", lambda e, b=b: e.tensor_copy(out=RSTD[:], in_=SQ[b][:].bitcast(F32)),
                                   reads=[sq_r[b]], writes=[rstd_r])
                        else:
                            rec.op("dve", lambda e, b=b: e.tensor_tensor(out=RSTD[:], in0=RSTD[:], in1=SQ[b][:].bitcast(F32),
                                                                         op=ALU.add),
                                   reads=[sq_r[b], rstd_r], writes=[rstd_r])
                        self.ksq += 1

            def finish(self):
                assert self.ksq == NK
                bank = next_stat_bank()
                rec.op("dve", lambda e: e.tensor_copy(out=SQ[0][:], in_=RSTD[:]), reads=[rstd_r], writes=[sq_r[0]])
                rec.op("pe", lambda e: e.matmul(PS[bank][:], ONES[:], SQ[0][:], start=True, stop=True),
                       reads=[sq_r[0], ones_r], writes=[ps_res[bank]])
                rec.op("act", lambda e: e.activation(out=RSTD[:], in_=PS[bank][:], func=AF.Ln, scale=1.0 / D, bias=pcol(C_EPS)),
                       reads=[ps_res[bank], prm_r], writes=[rstd_r])
                rec.op("act", lambda e: e.activation(out=RSTD[:], in_=RSTD[:], func=AF.Exp, scale=-0.5),
                       reads=[rstd_r], writes=[rstd_r])

            def scales(self, n):
                for _ in range(n):
                    if self.ksc < NK:
                        k = self.ksc
                        scale_to(self.reg.r(k), self.reg.res[k], self.reg.c(k), self.reg.res[k], self.gcol0 + k)
                        self.ksc += 1

            def all(self):
                self.squares(NK)
                self.finish()
                self.scales(NK)

        def proj_group(wd, idx, rhs, rhs_res, bank, ncol=T):
            for hf in range(2):
                row = (idx * 2 + hf) * 128
                wt, wr = wload(wd[row:row + 128, :], 2048)
                pairs = [(wt[:, kc * 128:(kc + 1) * 128], rhs(hf * 16 + kc)) for kc in range(16)]

                def fn(e, pairs=pairs, hf=hf, bank=bank, ncol=ncol):
                    ins = None
                    for i, (l, r) in enumerate(pairs):
                        ins = e.matmul(PS[bank][:, 0:ncol], l, r, start=(hf == 0 and i == 0), stop=(hf == 1 and i == 15))
                    return ins
                rec.op("pe", fn, reads=[wr] + list(rhs_res[hf * 16:(hf + 1) * 16]), writes=[ps_res[bank]])

        def win_panel_group(c, reg, bank, ncol=T, rhs=None):
            proj_group(win_d, c, rhs if rhs is not None else reg.r, reg.res, bank, ncol=ncol)

        def rnn_A(h, mix=v3.mix, alt=False, reg=None):
            XR_, XC_, XCr_, nXR, nXC = (V, ZG[0], ZGr[0], "V", "ZG0") if alt else (XR, XC, XCr, "XR", "XC")
            return _rnn_A(h, mix, XR_, XC_, XCr_, nXR, nXC, reg if reg is not None else RA)

        def _rnn_A(h, mix, XR, XC, XCr, nXR, nXC, reg):
            bank = next_bank()
            win_panel_group(h, reg, bank)
            rec.op("pool", lambda e: e.tensor_copy(out=XR[:, 0:3], in_=HALO[:, h * 3:h * 3 + 3]),
                   reads=[halo_r[h]], writes=[mix[nXR]])
            rec.op("act", lambda e: e.activation(out=XR[:, 3:515], in_=PS[bank][:], func=AF.Copy),
                   reads=[ps_res[bank]], writes=[mix[nXR]])
            rec.op("pool", lambda e: e.tensor_copy(out=HALO[:, h * 3:h * 3 + 3], in_=XR[:, 512:515]),
                   reads=[mix[nXR]], writes=[halo_r[h]])
            rec.op("dve", lambda e: e.tensor_scalar(out=XC, in0=XR[:, 0:512], scalar1=pcol(C_CONVW + 0 * 16 + h),
                                                    scalar2=pcol(C_CONVB + h), op0=ALU.mult, op1=ALU.add),
                   reads=[mix[nXR], prm_r], writes=[mix[nXC]])
            for k in (1, 2):
                rec.op("dve", lambda e, k=k: e.scalar_tensor_tensor(out=XC, in0=XR[:, k:k + 512],
                                                                    scalar=pcol(C_CONVW + k * 16 + h), in1=XC,
                                                                    op0=ALU.mult, op1=ALU.add),
                       reads=[mix[nXR], mix[nXC], prm_r], writes=[mix[nXC]])
            rec.op("dve", lambda e: e.scalar_tensor_tensor(out=XCr, in0=XR[:, 3:515],
                                                           scalar=pcol(C_CONVW + 3 * 16 + h), in1=XC,
                                                           op0=ALU.mult, op1=ALU.add),
                   reads=[mix[nXR], mix[nXC], prm_r], writes=[mix[nXC]])

        def rnn_B(h, mix=v3.mix, tapcol=None, alt=False):
            XC_, XCr_, nXC = (ZG[0], ZGr[0], "ZG0") if alt else (XC, XCr, "XC")
            return _rnn_B(h, mix, tapcol, XC_, XCr_, nXC)

        def _rnn_B(h, mix, tapcol, XC, XCr, nXC):
            wt, wr = wload_small(wg_d[h * 128:(h + 1) * 128, :])
            if tapcol is not None:
                tap1("xc", h * 128, XC, mix[nXC], tapcol)
            ba, bx = next_bank(), next_bank()
            mm_group(ba, [(wt[:, 0:128], XCr)], [wr, mix[nXC]])
            mm_group(bx, [(wt[:, 128:256], XCr)], [wr, mix[nXC]])
            rec.op("act", lambda e: e.activation(out=TA, in_=PS[ba][:], func=AF.Exp, bias=dcol(32 + h), scale=-1.0),
                   reads=[ps_res[ba], der_r], writes=[mix["TA"]])
            rec.op("act", lambda e: e.activation(out=TA, in_=TA, func=AF.Ln, bias=pcol(C_ONE), scale=1.0),
                   reads=[mix["TA"], prm_r], writes=[mix["TA"]])
            rec.op("act", lambda e: e.activation(out=TA, in_=TA, func=AF.Exp, scale=-1.0),
                   reads=[mix["TA"]], writes=[mix["TA"]])
            rec.op("act", lambda e: e.activation(out=A2, in_=TA, func=AF.Exp, bias=pcol(C_NEG), scale=dcol(48 + 16 + h)),
                   reads=[mix["TA"], der_r, prm_r], writes=[mix["A2"]])
            rec.op("act", lambda e: e.activation(out=TA, in_=TA, func=AF.Exp, scale=dcol(16 + h)),
                   reads=[mix["TA"], der_r], writes=[mix["TA"]])
            rec.op("act", lambda e: e.activation(out=A2, in_=A2, func=AF.Ln, bias=pcol(C_ONE), scale=-1.0),
                   reads=[mix["A2"], prm_r], writes=[mix["A2"]])
            rec.op("act", lambda e: e.activation(out=TX, in_=PS[bx][:], func=AF.Exp, bias=dcol(48 + h), scale=-1.0),
                   reads=[ps_res[bx], der_r], writes=[mix["TX"]])
            rec.op("act", lambda e: e.activation(out=TX, in_=TX, func=AF.Ln, bias=pcol(C_ONE), scale=1.0),
                   reads=[mix["TX"], prm_r], writes=[mix["TX"]])
            rec.op("dve", lambda e: e.scalar_tensor_tensor(out=BB, in0=A2, scalar=0.5, in1=TX, op0=ALU.mult, op1=ALU.subtract),
                   reads=[mix["A2"], mix["TX"]], writes=[mix["BB"]])
            rec.op("act", lambda e: e.activation(out=BB, in_=BB, func=AF.Exp),
                   reads=[mix["BB"]], writes=[mix["BB"]])
            rec.op("dve", lambda e: e.tensor_tensor(out=BB, in0=BB, in1=XC, op=ALU.mult),
                   reads=[mix["BB"], mix[nXC]], writes=[mix["BB"]])
            rec.op("dve", lambda e: e.tensor_tensor_scan(out=HS, data0=TA, data1=BB, initial=STATE[:, h:h + 1],
                                                         op0=ALU.mult, op1=ALU.add),
                   reads=[mix["TA"], mix["BB"], state_r[h]], writes=[mix["HS"]])
            rec.op("pool", lambda e: e.tensor_copy(out=STATE[:, h:h + 1], in_=HS[:, T - 1:T]),
                   reads=[mix["HS"]], writes=[state_r[h]])
            if tapcol is not None:
                tap1("a", h * 128, TA, mix["TA"], tapcol)
                tap1("b", h * 128, BB, mix["BB"], tapcol)
                tap1("hs", h * 128, HS, mix["HS"], tapcol)

        def rnn_C(h, statA, mix=v3.mix):
            bank = next_bank()
            win_panel_group(16 + h, RA, bank)
            P = PS[bank][:]
            rec.op("act", lambda e: e.activation(out=GQ, in_=P, func=AF.Square), reads=[ps_res[bank]], writes=[mix["GQ"]])
            rec.op("dve", lambda e: e.tensor_scalar(out=GQ, in0=GQ, scalar1=0.044715, scalar2=1.0, op0=ALU.mult, op1=ALU.add),
                   reads=[mix["GQ"]], writes=[mix["GQ"]])
            rec.op("dve", lambda e: e.tensor_tensor(out=GQ, in0=GQ, in1=P, op=ALU.mult),
                   reads=[mix["GQ"], ps_res[bank]], writes=[mix["GQ"]])
            rec.op("act", lambda e: e.activation(out=GQ, in_=GQ, func=AF.Exp, scale=-2.0 * 0.7978845608028654),
                   reads=[mix["GQ"]], writes=[mix["GQ"]])
            rec.op("act", lambda e: e.activation(out=GQ, in_=GQ, func=AF.Ln, bias=pcol(C_ONE), scale=1.0),
                   reads=[mix["GQ"], prm_r], writes=[mix["GQ"]])
            rec.op("act", lambda e: e.activation(out=GQ, in_=GQ, func=AF.Exp, scale=-1.0),
                   reads=[mix["GQ"]], writes=[mix["GQ"]])
            rec.op("dve", lambda e: e.tensor_tensor(out=GQ, in0=GQ, in1=P, op=ALU.mult),
                   reads=[mix["GQ"], ps_res[bank]], writes=[mix["GQ"]])
            return bank

        def rnn_C2(h, statA, mix=v3.mix):
            rec.op("dve", lambda e: e.tensor_tensor(out=R1c(h), in0=GQ, in1=HS, op=ALU.mult),
                   reads=[mix["GQ"], mix["HS"]], writes=[r1[h]])
            statA.square(R1c(h), [r1[h]])

        def pool_V(c, first_tile, mix=v3.mix):
            g = c // 4
            bank = next_bank()
            win_panel_group(32 + c, RA, bank)
            rec.op("pool", lambda e: e.tensor_copy(out=V[:, 0:16], in_=VHALO[:, c * 16:(c + 1) * 16]),
                   reads=[vhalo_r[c]], writes=[mix["V"]])
            rec.op("act", lambda e: e.activation(out=V[:, 16:528], in_=PS[bank][:], func=AF.Copy),
                   reads=[ps_res[bank]], writes=[mix["V"]])
            rec.op("pool", lambda e: e.tensor_copy(out=VHALO[:, c * 16:(c + 1) * 16], in_=V[:, 512:528]),
                   reads=[mix["V"]], writes=[vhalo_r[c]])
            srcs = [(V, "V"), (SA, "SA"), (SBt, "SB"), (SA, "SA"), (SBt, "SB")]
            cur, curn = V, "V"
            sh = 1
            for lvl in range(g + 1):
                dst, dstn = srcs[lvl + 1]
                lo = 2 * sh - 1
                rec.op("dve", lambda e, dst=dst, cur=cur, lo=lo, sh=sh: e.tensor_tensor(
                    out=dst[:, lo:528], in0=cur[:, lo:528], in1=cur[:, lo - sh:528 - sh], op=ALU.add),
                    reads=[mix[curn]], writes=[mix[dstn]])
                cur, curn = dst, dstn
                sh *= 2
            w = 2 ** (g + 1)
            zi = c % 4
            rec.op("dve", lambda e, cur=cur: e.scalar_tensor_tensor(out=ZGr[zi], in0=cur[:, 16:528], scalar=1.0 / w,
                                                                     in1=V[:, 16:528], op0=ALU.mult, op1=ALU.subtract),
                   reads=[mix[curn], mix["V"]], writes=[mix[f"ZG{zi}"]])
            if first_tile:
                rec.op("dve", lambda e, cur=cur: e.tensor_tensor(out=TMP16[:], in0=cur[:, 16:32],
                                                                  in1=PRM[:, C_INVCNT + g * 16:C_INVCNT + (g + 1) * 16],
                                                                  op=ALU.mult),
                       reads=[mix[curn], prm_r], writes=[tmp16_r])
                rec.op("dve", lambda e: e.tensor_tensor(out=ZGr[zi][:, 0:16], in0=TMP16[:], in1=V[:, 16:32],
                                                        op=ALU.subtract),
                       reads=[tmp16_r, mix["V"]], writes=[mix[f"ZG{zi}"]])

        def pool_LIN(g, statB, mix=v3.mix):
            wt, wr = wload(wpool_d[g * 128:(g + 1) * 128, :], 2048)
            for jo in range(4):
                bank = next_bank()
                pairs = [(wt[:, ki * 512 + jo * 128: ki * 512 + (jo + 1) * 128], ZGr[ki]) for ki in range(4)]
                mm_group(bank, pairs, [wr] + [mix[f"ZG{ki}"] for ki in range(4)])
                c = g * 4 + jo
                rec.op("act", lambda e, bank=bank, c=c: e.activation(out=R1c(16 + c), in_=PS[bank][:], func=AF.Identity,
                                                                      bias=pcol(C_BPOOL + c)),
                       reads=[ps_res[bank], prm_r], writes=[r1[16 + c]])
                statB.square(R1c(16 + c), [r1[16 + c]])

        def light_pass():
            tiles = [(slot, i) for slot in range(NPRIOR) for i in range(NT)]
            L = len(tiles)
            regs = [RA, RB]

            def src(n):
                return (xp_d, tiles[n][0] * D, tiles[n][1] * T) if n < L else (xm_d, 0, 0)
            r3_switch("mix")
            reg0 = regs[L % 2]
            load_x(reg0, *src(0))
            NormJob(reg0, C_GMIX).all()
            for n in range(L):
                slot, i = tiles[n]
                reg, nxt = regs[(L - n) % 2], regs[(L - n - 1) % 2]
                job = NormJob(nxt, C_GMIX)
                for h in range(16):
                    rnn_A(h, alt=(h % 2 == 1), reg=reg)
                    if h >= 1:
                        rnn_B(h - 1, alt=((h - 1) % 2 == 1))
                    if h < 8:
                        load_x(nxt, *src(n + 1), queue="act", ks=range(4 * h, 4 * h + 4))
                    if 2 <= h <= 9:
                        job.squares(4)
                    if h == 10:
                        job.finish()
                    if h >= 10:
                        job.scales(6 if h < 12 else 5)
                rnn_B(15, alt=True)
                if slot == NPRIOR - 1 and i == NT - 1:
                    for c in range(16):
                        bank = next_bank()
                        win_panel_group(32 + c, reg, bank, ncol=16, rhs=lambda k, reg=reg: reg.r(k)[:, T - 16:T])
                        rec.op("dve", lambda e, bank=bank, c=c: e.tensor_scalar(
                            out=VHALO[:, c * 16:(c + 1) * 16], in0=PS[bank][:, 0:16], scalar1=pcol(C_MASK + slot),
                            scalar2=None, op0=ALU.mult), reads=[ps_res[bank], prm_r], writes=[vhalo_r[c]])
                if i == NT - 1:
                    rec.op("dve", lambda e, slot=slot: e.tensor_scalar(out=STATE[:], in0=STATE[:], scalar1=pcol(C_MASK + slot),
                                                                       scalar2=None, op0=ALU.mult),
                           reads=state_r + [prm_r], writes=state_r)
                    rec.op("dve", lambda e, slot=slot: e.tensor_scalar(out=HALO[:], in0=HALO[:], scalar1=pcol(C_MASK + slot),
                                                                       scalar2=None, op0=ALU.mult),
                           reads=halo_r + [prm_r], writes=halo_r)

        def main_tile(i, prefetched=False):
            col0 = i * T
            r3_switch("mix")
            if not prefetched:
                load_x(RA, xm_d, 0, col0)
                NormJob(RA, C_GMIX).all()
            tap("u1", R2c, r2, col0)
            statA = Stat(16)
            for h in range(16):
                rnn_A(h)
                rnn_C(h, statA)
                statA.flush()
                rnn_B(h, tapcol=col0)
                rnn_C2(h, statA)
            statB = Stat(16)
            for c in range(16):
                pool_V(c, first_tile=(i == 0))
                if c % 4 == 3:
                    pool_LIN(c // 4, statB)
            for k in range(NK):
                rec.dma(R2c(k), xm_d[k * 128:(k + 1) * 128, col0:col0 + T], reads=[], writes=[r2[k]], sem=r2_sem[k])
            statA.finish(DR)
            for h in range(16):
                scale_to(R1r(h), r1[h], R1c(h), r1[h], C_BETA + h)
            statB.finish(DP)
            for c in range(16):
                scale_to(R1r(16 + c), r1[16 + c], R1c(16 + c), r1[16 + c], C_PSCALE + c)

            tap("mix", R1c, r1, col0)
            def big_proj(wd, m, rhs, rhs_res):
                bank = next_bank()
                proj_group(wd, m, rhs, rhs_res, bank)
                return bank

            st2 = Stat(NK)
            for m in range(NK):
                bank = big_proj(wout_d, m, R1r, r1)
                rec.op("dve", lambda e, bank=bank, m=m: e.tensor_tensor(out=R2c(m), in0=PS[bank][:], in1=R2c(m), op=ALU.add),
                       reads=[ps_res[bank], r2[m]], writes=[r2[m]])
                st2.square(R2c(m), [r2[m]])
            st2.finish(D)
            tap("h1", R2c, r2, col0)
            for k in range(NK):
                scale_to(R1r(k), r1[k], R2c(k), r2[k], C_GMLP + k)

            r3_switch("hb")

            def mlp_up(grp):
                b = grp % 2
                for j in range(8):
                    f = grp * 8 + j
                    bank = big_proj(wup_d, f, R1r, r1)
                    rec.op("act", lambda e, bank=bank, b=b, j=j: e.activation(out=HB[b][j], in_=PS[bank][:], func=AF.Relu),
                           reads=[ps_res[bank]], writes=[v3.hb[b][j]])
                    rec.op("dve", lambda e, b=b, j=j: e.tensor_tensor(out=HBr[b][j], in0=HB[b][j], in1=HB[b][j], op=ALU.mult),
                           reads=[v3.hb[b][j]], writes=[v3.hb[b][j]])

            st3 = Stat(NK)

            def mlp_down(grp):
                b = grp % 2
                last = grp == NGRP - 1
                for mp in range(16):
                    row = (grp * 16 + mp) * 128
                    wt, wr = wload(wdn_d[row:row + 128, :], 2048)
                    for mm_ in range(2):
                        m = mp * 2 + mm_
                        bank = next_bank()
                        pairs = [(wt[:, (mm_ * 8 + kc) * 128:(mm_ * 8 + kc + 1) * 128], HBr[b][kc]) for kc in range(8)]
                        mm_group(bank, pairs, [wr] + v3.hb[b])
                        rec.op("dve", lambda e, bank=bank, m=m: e.tensor_tensor(out=R2c(m), in0=PS[bank][:], in1=R2c(m), op=ALU.add),
                               reads=[ps_res[bank], r2[m]], writes=[r2[m]])
                        if last:
                            st3.square(R2c(m), [r2[m]])

            mlp_up(0)
            for grp in range(NGRP):
                if grp + 1 < NGRP:
                    mlp_up(grp + 1)
                mlp_down(grp)
            st3.finish(D)
            tap("h2", R2c, r2, col0)
            for k in range(NK):
                scale_to(R1r(k), r1[k], R2c(k), r2[k], C_GPLE + k)

            r3_switch("ple")
            for kp in range(2):
                rec.dma(PTr[kp], pt_d[kp * 128:(kp + 1) * 128, col0:col0 + T].bitcast(F32R), reads=[],
                        writes=[v3.ple["PT"]], sem=pt_sem, cast=True)
            st4 = Stat(NK)
            for m in range(NK):
                bg = big_proj(wgate_d, m, R1r, r1)
                wt, wr = wload_small(wproj_d[m * 128:(m + 1) * 128, :])
                bp = next_bank()
                mm_group(bp, [(wt[:, kp * 128:(kp + 1) * 128], PTr[kp]) for kp in range(2)], [wr, v3.ple["PT"]])
                rec.op("act", lambda e, bg=bg: e.activation(out=TG, in_=PS[bg][:], func=AF.Exp, scale=-1.0),
                       reads=[ps_res[bg]], writes=[v3.ple["TG"]])
                rec.op("act", lambda e: e.activation(out=TG, in_=TG, func=AF.Ln, bias=pcol(C_ONE), scale=1.0),
                       reads=[v3.ple["TG"], prm_r], writes=[v3.ple["TG"]])
                rec.op("act", lambda e: e.activation(out=TG, in_=TG, func=AF.Exp, scale=-1.0),
                       reads=[v3.ple["TG"]], writes=[v3.ple["TG"]])
                rec.op("dve", lambda e, bp=bp: e.tensor_tensor(out=TG2, in0=TG, in1=PS[bp][:], op=ALU.mult),
                       reads=[v3.ple["TG"], ps_res[bp]], writes=[v3.ple["TG2"]])
                rec.op("dve", lambda e, m=m: e.tensor_tensor(out=R2c(m), in0=TG2, in1=R2c(m), op=ALU.add),
                       reads=[v3.ple["TG2"], r2[m]], writes=[r2[m]])
                st4.square(R2c(m), [r2[m]])
            st4.finish(D)
            tap("h3", R2c, r2, col0)
            for m in range(NK):
                scale_to(R2c(m), r2[m], R2c(m), r2[m], C_GFIN + m)
                rec.dma(out_d[m * 128:(m + 1) * 128, col0:col0 + T], R2c(m), reads=[r2[m]], writes=[], sem=r2_sem[m])

        light_pass()
        for i in range(NT):
            main_tile(i, prefetched=(i == 0))
        rec.final_wait("sp", r2_sem + r1_sem + [tap_sem])

        with nc.Block() as block:
            @block.tensor
            def _(e):
                rec.replay(nc, e, "pe")

            @block.scalar
            def _(e):
                rec.replay(nc, e, "act")

            @block.vector
            def _(e):
                rec.replay(nc, e, "dve")

            @block.gpsimd
            def _(e):
                rec.replay(nc, e, "pool")

            @block.sync
            def _(e):
                rec.replay(nc, e, "sp")
    return nc


def _chunked(v, n):
    return np.ascontiguousarray(np.asarray(v, np.float32).reshape(n, 128).T)


def _panel_k2(w):
    K, M = w.shape
    assert K == 4096
    a = w.reshape(2, 16, 128, M // 128, 128)
    a = a.transpose(3, 0, 2, 1, 4)
    return np.ascontiguousarray(a).reshape(-1, 2048)


def prep_weights(w_in, w_rg_a, w_rg_x, w_pool, w_out, w_up, w_down, w_ple_gate, w_ple_proj):
    DFF = w_up.shape[1]
    ngrp = DFF // 1024
    d = {}
    d["win"] = _panel_k2(w_in)
    wg = np.concatenate([w_rg_a, w_rg_x], axis=2)
    d["wg"] = np.ascontiguousarray(wg).reshape(16 * 128, 256)
    wp = w_pool.reshape(4, 4, 128, 512).transpose(0, 2, 1, 3)
    d["wpool"] = np.ascontiguousarray(wp).reshape(4 * 128, 2048)
    d["wout"] = _panel_k2(w_out)
    d["wup"] = _panel_k2(w_up)
    a = w_down.reshape(ngrp, 8, 128, 16, 2, 128)
    a = a.transpose(0, 3, 2, 4, 1, 5)
    d["wdn"] = np.ascontiguousarray(a).reshape(ngrp * 16 * 128, 2048)
    d["wgate"] = _panel_k2(w_ple_gate)
    a = w_ple_proj.reshape(2, 128, 32, 128).transpose(2, 1, 0, 3)
    d["wproj"] = np.ascontiguousarray(a).reshape(32 * 128, 256)
    return d


def prep_prm(q, norm_mix_g, norm_mlp_g, norm_ple_g, final_norm_g, conv_w, conv_b, b_rg_a, b_rg_x, lru_lambda,
             beta_rnn, b_pool, pool_scale, nprior=3):
    prm = np.zeros((128, NPRM), np.float32)
    prm[:, C_GMIX:C_GMIX + 32] = _chunked(norm_mix_g, 32)
    prm[:, C_GMLP:C_GMLP + 32] = _chunked(norm_mlp_g, 32)
    prm[:, C_GPLE:C_GPLE + 32] = _chunked(norm_ple_g, 32)
    prm[:, C_GFIN:C_GFIN + 32] = _chunked(final_norm_g, 32)
    for k in range(4):
        prm[:, C_CONVW + k * 16:C_CONVW + (k + 1) * 16] = _chunked(conv_w[k], 16)
    prm[:, C_CONVB:C_CONVB + 16] = _chunked(conv_b, 16)
    prm[:, C_BA:C_BA + 16] = _chunked(b_rg_a, 16)
    prm[:, C_BX:C_BX + 16] = _chunked(b_rg_x, 16)
    prm[:, C_LAM:C_LAM + 16] = _chunked(lru_lambda, 16)
    prm[:, C_BETA:C_BETA + 16] = _chunked(beta_rnn, 16)
    prm[:, C_BPOOL:C_BPOOL + 16] = _chunked(b_pool, 16)
    prm[:, C_PSCALE:C_PSCALE + 16] = _chunked(pool_scale, 16)
    prm[:, C_EPS] = EPS
    prm[:, C_ONE] = 1.0
    prm[:, C_NEG] = -1e-7
    for j in range(nprior):
        prm[:, C_MASK + j] = 1.0 if (q - nprior + j) >= 0 else 0.0
    for g in range(4):
        w = 2 ** (g + 1)
        for t in range(16):
            cnt = min(t + 1, w) if q == 0 else w
            prm[:, C_INVCNT + g * 16 + t] = np.float32(1.0) / np.float32(cnt)
    return prm


def run(inputs, NT, trace=False, debug_taps=False):
    x = np.asarray(inputs["x"], np.float32)
    p = np.asarray(inputs["p"], np.float32)[0]
    B, S, _ = x.shape
    QPB = NCORE // B
    SEG = S // QPB
    assert SEG == NT * T
    NPRIOR = QPB - 1
    DFF = inputs["w_up"].shape[2]
    wd = prep_weights(inputs["w_in"][0], inputs["w_rg_a"][0], inputs["w_rg_x"][0], inputs["w_pool"][0],
                      inputs["w_out"][0], inputs["w_up"][0], inputs["w_down"][0], inputs["w_ple_gate"][0],
                      inputs["w_ple_proj"][0])
    wd = {k: np.asarray(v, np.float32) for k, v in wd.items()}
    nc = build_nc(NT, DFF, NPRIOR, debug_taps=debug_taps)
    in_maps = []
    xT = [np.ascontiguousarray(x[b].T) for b in range(B)]
    pT = [np.ascontiguousarray(p[b].T) for b in range(B)]
    for c in range(NCORE):
        b, q = divmod(c, QPB)
        xm = np.ascontiguousarray(xT[b][:, q * SEG:(q + 1) * SEG])
        xp = np.zeros((NPRIOR * D, SEG), np.float32)
        for j in range(NPRIOR):
            sidx = q - NPRIOR + j
            if sidx >= 0:
                xp[j * D:(j + 1) * D] = xT[b][:, sidx * SEG:(sidx + 1) * SEG]
        prm = prep_prm(q, inputs["norm_mix_g"][0], inputs["norm_mlp_g"][0], inputs["norm_ple_g"][0],
                       inputs["final_norm_g"], inputs["conv_w"][0], inputs["conv_b"][0], inputs["b_rg_a"][0],
                       inputs["b_rg_x"][0], inputs["lru_lambda"][0], inputs["beta_rnn"][0], inputs["b_pool"][0],
                       inputs["pool_scale"][0], nprior=NPRIOR)
        m = {"xm": xm, "xp": xp, "pt": np.ascontiguousarray(pT[b][:, q * SEG:(q + 1) * SEG]), "prm": prm}
        m.update(wd)
        in_maps.append(m)
    res = run_bass_kernel_spmd(nc, in_maps, core_ids=list(range(NCORE)), trace=trace)
    out = np.empty((B, S, D), np.float32)
    for c in range(NCORE):
        b, q = divmod(c, QPB)
        out[b, q * SEG:(q + 1) * SEG, :] = res.results[c]["out"].T
    if debug_taps:
        taps = {}
        for n in ("u1", "mix", "h1", "h2", "h3", "xc", "a", "b", "hs"):
            a = np.empty((B, S, D), np.float32)
            for c in range(NCORE):
                b, q = divmod(c, QPB)
                a[b, q * SEG:(q + 1) * SEG, :] = res.results[c]["tap_" + n].T
            taps[n] = a
        return out, res, taps
    return out, res


def kernel(**inputs):
    out, _ = run(inputs, NT=4)
    return out
```

```python
import numpy as np
import concourse.bass as bass
import concourse.mybir as mybir
from concourse.bass_utils import run_bass_kernel_spmd
from contextlib import ExitStack

F32 = mybir.dt.float32
F32R = mybir.dt.float32r
AF = mybir.ActivationFunctionType
ALU = mybir.AluOpType

D = 4096
NK = D // 128
DR = 2048
DP = 2048
PLE = 256
T = 512
NCORE = 8
EPS = 1e-6
LRU_C = 8.0

C_GMIX = 0
C_GMLP = 32
C_GPLE = 64
C_GFIN = 96
C_CONVW = 128
C_CONVB = 192
C_BA = 208
C_BX = 224
C_LAM = 240
C_BETA = 256
C_BPOOL = 272
C_PSCALE = 288
C_MASK = 304
C_INVCNT = 307
C_EPS = 371
C_ONE = 372
C_NEG = 373
NPRM = 374


class Sem:
    def __init__(self, h, name):
        self.h = h
        self.name = name
        self.count = 0


class Res:
    __slots__ = ("name", "w", "r")

    def __init__(self, name):
        self.name = name
        self.w = None
        self.r = {}


class Stream:
    def __init__(self, name, sem):
        self.name = name
        self.sem = sem
        self.ops = []
        self.seen = {}


class Rec:
    def __init__(self):
        self.streams = {}

    def add_stream(self, name, sem):
        self.streams[name] = Stream(name, sem)

    def _waits(self, st, reads, writes):
        need = {}

        def add(ev, ordered_ok):
            if ev is None:
                return
            sem, val = ev
            if ordered_ok and sem is st.sem:
                return
            if need.get(sem, 0) < val:
                need[sem] = val

        for r in reads:
            add(r.w, False)
        for w in writes:
            add(w.w, True)
            for sem, val in w.r.items():
                add((sem, val), True)
        for sem, val in need.items():
            if st.seen.get(sem, 0) < val:
                st.seen[sem] = val
                st.ops.append(("wait", sem, val))

    def _mark(self, ev, reads, writes):
        sem, val = ev
        for r in reads:
            if r.r.get(sem, 0) < val:
                r.r[sem] = val
        for w in writes:
            w.w = ev
            w.r = {}

    def op(self, stname, fn, reads=(), writes=(), inc=True):
        st = self.streams[stname]
        self._waits(st, reads, writes)
        if inc:
            st.sem.count += 1
            ev = (st.sem, st.sem.count)
            st.ops.append(("opi", fn))
        else:
            ev = (st.sem, st.sem.count + 1)
            st.ops.append(("op", fn))
        self._mark(ev, reads, writes)

    def dma(self, out_ap, in_ap, reads, writes, sem, cast=False, queue="sp"):
        st = self.streams[queue]
        self._waits(st, reads, writes)
        sem.count += 16
        ev = (sem, sem.count)
        st.ops.append(("dma", out_ap, in_ap, sem, cast))
        self._mark(ev, reads, writes)

    def final_wait(self, stname, sems):
        st = self.streams[stname]
        for s in sems:
            if s.count > 0:
                st.ops.append(("wait", s, s.count))

    def replay(self, nc, eng, stname):
        st = self.streams[stname]
        for o in st.ops:
            k = o[0]
            if k == "wait":
                eng.wait_ge(o[1].h, o[2])
            elif k == "opi":
                o[1](eng).then_inc(st.sem.h, 1)
            elif k == "op":
                o[1](eng)
            else:
                _, out_ap, in_ap, sem, cast = o
                nc.dge_precook = not cast
                eng.dma_start(out=out_ap, in_=in_ap).then_inc(sem.h, 16)
        nc.dge_precook = True


def alias_barrier(new_res, old_res):
    acc = {}
    for o in old_res:
        if o.w is not None:
            s, v = o.w
            if acc.get(s, 0) < v:
                acc[s] = v
        for s, v in o.r.items():
            if acc.get(s, 0) < v:
                acc[s] = v
    for n in new_res:
        n.w = None
        n.r = dict(acc)


def build_nc(NT, DFF, NPRIOR=3, debug_taps=False):
    NF = DFF // 128
    NGRP = NF // 8
    SEG = NT * T
    nc = bass.Bass("TRN2", target_bir_lowering=False)

    def dram(name, shape, kind="ExternalInput"):
        return nc.dram_tensor(name, shape, F32, kind=kind).ap()

    xm_d = dram("xm", [D, SEG])
    xp_d = dram("xp", [NPRIOR * D, SEG])
    pt_d = dram("pt", [PLE, SEG])
    prm_d = dram("prm", [128, NPRM])
    win_d = dram("win", [48 * 2 * 128, 2048])
    wg_d = dram("wg", [16 * 128, 256])
    wpool_d = dram("wpool", [4 * 128, 2048])
    wout_d = dram("wout", [32 * 2 * 128, 2048])
    wup_d = dram("wup", [NF * 2 * 128, 2048])
    wdn_d = dram("wdn", [NGRP * 16 * 128, 2048])
    wgate_d = dram("wgate", [32 * 2 * 128, 2048])
    wproj_d = dram("wproj", [32 * 128, 256])
    out_d = dram("out", [D, SEG], kind="ExternalOutput")
    TAPS = ("u1", "mix", "h1", "h2", "h3", "xc", "a", "b", "hs") if debug_taps else ()
    tap_d = {n: dram("tap_" + n, [D, SEG], kind="ExternalOutput") for n in TAPS}

    es = ExitStack()
    with es:
        def sb(name, cols, dt=F32):
            return es.enter_context(nc.sbuf_tensor(name, [128, cols], dt))

        def arena(name, cols):
            off = (nc.sbuf_base + 31) // 32 * 32
            t = sb(name, cols)
            tr = nc.alloc_sbuf_tensor_at(name + "r", [128, cols], F32R, offset=off)
            return t, tr

        R1, R1R = arena("R1", NK * T)
        R2, R2R = arena("R2", NK * T)
        R3, R3R = arena("R3", 8192)
        WS = [sb(f"W{i}", 2048, F32R) for i in range(4)]
        WSM = [sb(f"WSM{i}", 256, F32R) for i in range(4)]
        ONES = sb("ONES", 128, F32R)
        PRM = sb("PRM", NPRM)
        DER = sb("DER", 80)
        DTMP = sb("DTMP", 64)
        RSTD = sb("RSTD", T)
        HALO = sb("HALO", 16 * 3)
        VHALO = sb("VHALO", 16 * 16)
        STATE = sb("STATE", 16)
        TMP16 = sb("TMP16", 16)
        SQ = [sb("SQ0", T, F32R), sb("SQ1", T, F32R)]
        PS = [es.enter_context(nc.psum_tensor(f"ps{i}", [128, T], F32)) for i in range(8)]

        def sem(name):
            return Sem(es.enter_context(nc.semaphore(name)), name)

        rec = Rec()
        for n in ("pe", "act", "dve", "pool"):
            rec.add_stream(n, sem("s_" + n))
        rec.add_stream("sp", sem("s_sp"))
        w_sem = [sem(f"s_w{i}") for i in range(4)]
        wsm_sem = [sem(f"s_wsm{i}") for i in range(4)]
        r2_sem = [sem(f"s_r2_{i}") for i in range(NK)]
        r1_sem = [sem(f"s_r1_{i}") for i in range(NK)]
        pt_sem = sem("s_pt")
        prm_sem = sem("s_prm")

        r1 = [Res(f"r1_{k}") for k in range(NK)]
        r2 = [Res(f"r2_{k}") for k in range(NK)]
        w_res = [Res(f"w{i}") for i in range(4)]
        wsm_res = [Res(f"wsm{i}") for i in range(4)]
        ps_res = [Res(f"ps{i}") for i in range(8)]
        ones_r, prm_r, der_r, dtmp_r = Res("ones"), Res("prm"), Res("der"), Res("dtmp")
        rstd_r, negh_r, half_r = Res("rstd"), Res("negh"), Res("half")
        halo_r = [Res(f"halo{h}") for h in range(16)]
        vhalo_r = [Res(f"vhalo{h}") for h in range(16)]
        state_r = [Res(f"state{h}") for h in range(16)]
        tmp16_r, sq_r = Res("tmp16"), [Res("sq0"), Res("sq1")]

        def r3(off, n):
            return R3[:, off:off + n]

        def r3r(off, n):
            return R3R[:, off:off + n]
        XR = r3(0, 528)
        XC, TA, A2, TX, BB, HS, GQ = [r3(528 + i * T, T) for i in range(7)]
        o = 528 + 7 * T
        V, SA, SBt = r3(o, 528), r3(o + 528, 528), r3(o + 1056, 528)
        ZG = [r3(o + 1584 + i * T, T) for i in range(4)]
        ZGr = [r3r(o + 1584 + i * T, T) for i in range(4)]
        XCr = r3r(528, T)
        mix_names = ["XR", "XC", "TA", "A2", "TX", "BB", "HS", "GQ", "V", "SA", "SB", "ZG0", "ZG1", "ZG2", "ZG3"]
        HB = [[r3((b * 8 + j) * T, T) for j in range(8)] for b in range(2)]
        HBr = [[r3r((b * 8 + j) * T, T) for j in range(8)] for b in range(2)]
        PTr = [r3r(kp * T, T) for kp in range(2)]
        TG, TG2 = r3(2 * T, T), r3(3 * T, T)

        class V3:
            pass
        v3 = V3()
        v3.mix = {n: Res("m_" + n) for n in mix_names}
        v3.hb = [[Res(f"hb{b}_{j}") for j in range(8)] for b in range(2)]
        v3.ple = {n: Res("p_" + n) for n in ("PT", "TG", "TG2")}
        v3.cur = "mix"

        def r3_all(view):
            if view == "mix":
                return list(v3.mix.values())
            if view == "hb":
                return [x for row in v3.hb for x in row]
            return list(v3.ple.values())

        def r3_switch(view):
            if v3.cur != view:
                alias_barrier(r3_all(view), r3_all(v3.cur))
                v3.cur = view

        def R1c(k):
            return R1[:, k * T:(k + 1) * T]

        def R2c(k):
            return R2[:, k * T:(k + 1) * T]

        def R1r(k):
            return R1R[:, k * T:(k + 1) * T]

        def R2r(k):
            return R2R[:, k * T:(k + 1) * T]

        def pcol(c):
            return PRM[:, c:c + 1]

        def dcol(c):
            return DER[:, c:c + 1]

        tap_sem = sem("s_tap")

        def tap(name, chunk_ap, chunk_res, col0):
            if name not in tap_d:
                return
            for k in range(NK):
                rec.dma(tap_d[name][k * 128:(k + 1) * 128, col0:col0 + T], chunk_ap(k), reads=[chunk_res[k]], writes=[],
                        sem=tap_sem)

        def tap1(name, row0, ap, res, col0):
            if name in tap_d:
                rec.dma(tap_d[name][row0:row0 + 128, col0:col0 + T], ap, reads=[res], writes=[], sem=tap_sem)

        wstate = {"i": 0}

        def wload(dram_rows, ncols):
            s = wstate["i"] % 4
            wstate["i"] += 1
            rec.dma(WS[s][:, 0:ncols], dram_rows.bitcast(F32R), reads=[], writes=[w_res[s]], sem=w_sem[s], cast=True)
            return WS[s], w_res[s]

        def wload_small(dram_rows):
            s_ = wstate.get("j", 0) % 4
            wstate["j"] = wstate.get("j", 0) + 1
            rec.dma(WSM[s_][:, :], dram_rows.bitcast(F32R), reads=[], writes=[wsm_res[s_]], sem=wsm_sem[s_], cast=True)
            return WSM[s_], wsm_res[s_]

        pstate = {"i": 0, "s": 0, "q": 0}

        def next_bank():
            b = pstate["i"] % 6
            pstate["i"] += 1
            return b

        def next_stat_bank():
            b = 6 + pstate["s"] % 2
            pstate["s"] += 1
            return b

        def mm_group(bank, pairs, reads, ncol=T):
            n = len(pairs)

            def fn(e, pairs=pairs, bank=bank, n=n, ncol=ncol):
                ins = None
                for i, (l, r) in enumerate(pairs):
                    ins = e.matmul(PS[bank][:, 0:ncol], l, r, start=(i == 0), stop=(i == n - 1))
                return ins
            rec.op("pe", fn, reads=reads, writes=[ps_res[bank]])

        class Stat:
            def __init__(self, n):
                self.n = n
                self.i = 0
                self.bank = next_stat_bank()
                self.pending = False

            def square(self, src_ap, src_res):
                p = pstate.get("pend")
                if p is not None and p is not self:
                    p.flush()
                pstate["pend"] = self
                self.flush()
                b = pstate["q"] % 2
                pstate["q"] += 1
                self.buf = b
                rec.op("act", lambda e, s=src_ap, b=b: e.activation(out=SQ[b][:], in_=s, func=AF.Square),
                       reads=src_res, writes=[sq_r[b]])
                self.pending = True

            def flush(self):
                if self.pending:
                    self.pending = False
                    self.mm()

            def mm(self):
                i, n, bank, b = self.i, self.n, self.bank, self.buf
                self.i += 1
                rec.op("pe", lambda e: e.matmul(PS[bank][:], ONES[:], SQ[b][:], start=(i == 0), stop=(i == n - 1)),
                       reads=[sq_r[b], ones_r], writes=[ps_res[bank]])

            def finish(self, dim):
                self.flush()
                assert self.i == self.n
                bank = self.bank
                rec.op("act", lambda e: e.activation(out=RSTD[:], in_=PS[bank][:], func=AF.Ln, scale=1.0 / dim, bias=pcol(C_EPS)),
                       reads=[ps_res[bank], prm_r], writes=[rstd_r])
                rec.op("act", lambda e: e.activation(out=RSTD[:], in_=RSTD[:], func=AF.Exp, scale=-0.5),
                       reads=[rstd_r], writes=[rstd_r])

        def scale_to(dst_ap, dst_res, src_ap, src_res, gcol):
            d = dst_ap
            rec.op("dve", lambda e: e.scalar_tensor_tensor(out=d, in0=src_ap, scalar=pcol(gcol), in1=RSTD[:],
                                                           op0=ALU.mult, op1=ALU.mult),
                   reads=[src_res, rstd_r, prm_r], writes=[dst_res])

        rec.dma(PRM[:], prm_d[:, :], reads=[], writes=[prm_r], sem=prm_sem)
        rec.op("dve", lambda e: e.tensor_scalar(out=ONES[:], in0=PRM[:, 0:128], scalar1=0.0, scalar2=1.0,
                                                op0=ALU.mult, op1=ALU.add), reads=[prm_r], writes=[ones_r])
        rec.op("dve", lambda e: e.memset(HALO[:], 0.0), writes=halo_r)
        rec.op("dve", lambda e: e.memset(VHALO[:], 0.0), writes=vhalo_r)
        rec.op("dve", lambda e: e.memset(STATE[:], 0.0), writes=state_r)
        E_, U_, L_, D_ = DTMP[:, 0:16], DTMP[:, 16:32], DTMP[:, 32:48], DTMP[:, 48:64]
        rec.op("act", lambda e: e.activation(out=E_, in_=PRM[:, C_LAM:C_LAM + 16], func=AF.Exp, scale=-1.0),
               reads=[prm_r], writes=[dtmp_r])
        rec.op("dve", lambda e: e.tensor_scalar(out=U_, in0=E_, scalar1=1.0, scalar2=None, op0=ALU.add),
               reads=[dtmp_r], writes=[dtmp_r])
        rec.op("act", lambda e: e.activation(out=L_, in_=U_, func=AF.Ln), reads=[dtmp_r], writes=[dtmp_r])
        rec.op("dve", lambda e: e.tensor_scalar(out=D_, in0=U_, scalar1=-1.0, scalar2=1e-30, op0=ALU.add, op1=ALU.max),
               reads=[dtmp_r], writes=[dtmp_r])
        rec.op("dve", lambda e: e.reciprocal(out=D_, in_=D_), reads=[dtmp_r], writes=[dtmp_r])
        rec.op("dve", lambda e: e.tensor_tensor(out=L_, in0=L_, in1=E_, op=ALU.mult), reads=[dtmp_r], writes=[dtmp_r])
        rec.op("dve", lambda e: e.tensor_tensor(out=L_, in0=L_, in1=D_, op=ALU.mult), reads=[dtmp_r], writes=[dtmp_r])
        rec.op("dve", lambda e: e.tensor_scalar(out=DER[:, 0:16], in0=L_, scalar1=-0.5 * LRU_C, scalar2=None, op0=ALU.mult),
               reads=[dtmp_r], writes=[der_r])
        rec.op("dve", lambda e: e.tensor_scalar(out=DER[:, 16:32], in0=L_, scalar1=-LRU_C, scalar2=None, op0=ALU.mult),
               reads=[dtmp_r], writes=[der_r])
        rec.op("dve", lambda e: e.tensor_scalar(out=DER[:, 64:80], in0=L_, scalar1=-2.0 * LRU_C, scalar2=None, op0=ALU.mult),
               reads=[dtmp_r], writes=[der_r])
        rec.op("dve", lambda e: e.tensor_scalar(out=DER[:, 32:48], in0=PRM[:, C_BA:C_BA + 16], scalar1=-1.0, scalar2=None,
                                                op0=ALU.mult), reads=[prm_r], writes=[der_r])
        rec.op("dve", lambda e: e.tensor_scalar(out=DER[:, 48:64], in0=PRM[:, C_BX:C_BX + 16], scalar1=-1.0, scalar2=None,
                                                op0=ALU.mult), reads=[prm_r], writes=[der_r])

        class Reg:
            pass
        RA, RB = Reg(), Reg()
        RA.c, RA.r, RA.res, RA.sem = R2c, R2r, r2, r2_sem
        RB.c, RB.r, RB.res, RB.sem = R1c, R1r, r1, r1_sem

        def load_x(reg, src_d, row0, col0, queue="sp", ks=None):
            for k in (range(NK) if ks is None else ks):
                rec.dma(reg.c(k), src_d[row0 + k * 128: row0 + (k + 1) * 128, col0:col0 + T],
                        reads=[], writes=[reg.res[k]], sem=reg.sem[k], queue=queue)

        class NormJob:
            def __init__(self, reg, gcol0):
                self.reg, self.gcol0 = reg, gcol0
                self.st = None
                self.ksq = 0
                self.ksc = 0

            def squares(self, n):
                for _ in range(n):
                    if self.ksq < NK:
                        k = self.ksq
                        p = pstate.get("pend")
                        if p is not None:
                            p.flush()
                            pstate["pend"] = None
                        b = pstate["q"] % 2
                        pstate["q"] += 1
                        rec.op("act", lambda e, k=k, b=b: e.activation(out=SQ[b][:], in_=self.reg.c(k), func=AF.Square),
                               reads=[self.reg.res[k]], writes=[sq_r[b]])
                        if k == 0:
                            rec.op("dve", lambda e, b=b: e.tensor_copy(out=RSTD[:], in_=SQ[b][:].bitcast(F32)),
                                   reads=[sq_r[b]], writes=[rstd_r])
                        else:
                            rec.op("dve", lambda e, b=b: e.tensor_tensor(out=RSTD[:], in0=RSTD[:], in1=SQ[b][:].bitcast(F32),
                                                                         op=ALU.add),
                                   reads=[sq_r[b], rstd_r], writes=[rstd_r])
                        self.ksq += 1

            def finish(self):
                assert self.ksq == NK
                bank = next_stat_bank()
                rec.op("dve", lambda e: e.tensor_copy(out=SQ[0][:], in_=RSTD[:]), reads=[rstd_r], writes=[sq_r[0]])
                rec.op("pe", lambda e: e.matmul(PS[bank][:], ONES[:], SQ[0][:], start=True, stop=True),
                       reads=[sq_r[0], ones_r], writes=[ps_res[bank]])
                rec.op("act", lambda e: e.activation(out=RSTD[:], in_=PS[bank][:], func=AF.Ln, scale=1.0 / D, bias=pcol(C_EPS)),
                       reads=[ps_res[bank], prm_r], writes=[rstd_r])
                rec.op("act", lambda e: e.activation(out=RSTD[:], in_=RSTD[:], func=AF.Exp, scale=-0.5),
                       reads=[rstd_r], writes=[rstd_r])

            def scales(self, n):
                for _ in range(n):
                    if self.ksc < NK:
                        k = self.ksc
                        scale_to(self.reg.r(k), self.reg.res[k], self.reg.c(k), self.reg.res[k], self.gcol0 + k)
                        self.ksc += 1

            def all(self):
                self.squares(NK)
                self.finish()
                self.scales(NK)

        def proj_group(wd, idx, rhs, rhs_res, bank, ncol=T):
            for hf in range(2):
                row = (idx * 2 + hf) * 128
                wt, wr = wload(wd[row:row + 128, :], 2048)
                pairs = [(wt[:, kc * 128:(kc + 1) * 128], rhs(hf * 16 + kc)) for kc in range(16)]

                def fn(e, pairs=pairs, hf=hf, bank=bank, ncol=ncol):
                    ins = None
                    for i, (l, r) in enumerate(pairs):
                        ins = e.matmul(PS[bank][:, 0:ncol], l, r, start=(hf == 0 and i == 0), stop=(hf == 1 and i == 15))
                    return ins
                rec.op("pe", fn, reads=[wr] + list(rhs_res[hf * 16:(hf + 1) * 16]), writes=[ps_res[bank]])

        def win_panel_group(c, reg, bank, ncol=T, rhs=None):
            proj_group(win_d, c, rhs if rhs is not None else reg.r, reg.res, bank, ncol=ncol)

        ALT = [(XR, XC, XCr, "XR", "XC"), (V, ZG[0], ZGr[0], "V", "ZG0"), (SA, ZG[1], ZGr[1], "SA", "ZG1")]

        def rnn_A(h, mix=v3.mix, alt=0, reg=None):
            XR_, XC_, XCr_, nXR, nXC = ALT[int(alt)]
            return _rnn_A(h, mix, XR_, XC_, XCr_, nXR, nXC, reg if reg is not None else RA)

        def _rnn_A(h, mix, XR, XC, XCr, nXR, nXC, reg):
            bank = next_bank()
            win_panel_group(h, reg, bank)
            rec.op("pool", lambda e: e.tensor_copy(out=XR[:, 0:3], in_=HALO[:, h * 3:h * 3 + 3]),
                   reads=[halo_r[h]], writes=[mix[nXR]])
            rec.op("act", lambda e: e.activation(out=XR[:, 3:515], in_=PS[bank][:], func=AF.Copy),
                   reads=[ps_res[bank]], writes=[mix[nXR]])
            rec.op("pool", lambda e: e.tensor_copy(out=HALO[:, h * 3:h * 3 + 3], in_=XR[:, 512:515]),
                   reads=[mix[nXR]], writes=[halo_r[h]])
            rec.op("dve", lambda e: e.tensor_scalar(out=XC, in0=XR[:, 0:512], scalar1=pcol(C_CONVW + 0 * 16 + h),
                                                    scalar2=pcol(C_CONVB + h), op0=ALU.mult, op1=ALU.add),
                   reads=[mix[nXR], prm_r], writes=[mix[nXC]])
            for k in (1, 2):
                rec.op("dve", lambda e, k=k: e.scalar_tensor_tensor(out=XC, in0=XR[:, k:k + 512],
                                                                    scalar=pcol(C_CONVW + k * 16 + h), in1=XC,
                                                                    op0=ALU.mult, op1=ALU.add),
                       reads=[mix[nXR], mix[nXC], prm_r], writes=[mix[nXC]])
            rec.op("dve", lambda e: e.scalar_tensor_tensor(out=XCr, in0=XR[:, 3:515],
                                                           scalar=pcol(C_CONVW + 3 * 16 + h), in1=XC,
                                                           op0=ALU.mult, op1=ALU.add),
                   reads=[mix[nXR], mix[nXC], prm_r], writes=[mix[nXC]])

        def rnn_B(h, mix=v3.mix, tapcol=None, alt=0):
            _, XC_, XCr_, _, nXC = ALT[int(alt)]
            return _rnn_B(h, mix, tapcol, XC_, XCr_, nXC)

        def _rnn_B(h, mix, tapcol, XC, XCr, nXC):
            wt, wr = wload_small(wg_d[h * 128:(h + 1) * 128, :])
            if tapcol is not None:
                tap1("xc", h * 128, XC, mix[nXC], tapcol)
            ba, bx = next_bank(), next_bank()
            mm_group(ba, [(wt[:, 0:128], XCr)], [wr, mix[nXC]])
            mm_group(bx, [(wt[:, 128:256], XCr)], [wr, mix[nXC]])
            rec.op("act", lambda e: e.activation(out=TA, in_=PS[ba][:], func=AF.Exp, bias=dcol(32 + h), scale=-1.0),
                   reads=[ps_res[ba], der_r], writes=[mix["TA"]])
            rec.op("act", lambda e: e.activation(out=TA, in_=TA, func=AF.Ln, bias=pcol(C_ONE), scale=1.0),
                   reads=[mix["TA"], prm_r], writes=[mix["TA"]])
            rec.op("act", lambda e: e.activation(out=TA, in_=TA, func=AF.Exp, scale=-1.0),
                   reads=[mix["TA"]], writes=[mix["TA"]])
            rec.op("act", lambda e: e.activation(out=A2, in_=TA, func=AF.Exp, bias=pcol(C_NEG), scale=dcol(48 + 16 + h)),
                   reads=[mix["TA"], der_r, prm_r], writes=[mix["A2"]])
            rec.op("act", lambda e: e.activation(out=TA, in_=TA, func=AF.Exp, scale=dcol(16 + h)),
                   reads=[mix["TA"], der_r], writes=[mix["TA"]])
            rec.op("act", lambda e: e.activation(out=A2, in_=A2, func=AF.Ln, bias=pcol(C_ONE), scale=-1.0),
                   reads=[mix["A2"], prm_r], writes=[mix["A2"]])
            rec.op("act", lambda e: e.activation(out=TX, in_=PS[bx][:], func=AF.Exp, bias=dcol(48 + h), scale=-1.0),
                   reads=[ps_res[bx], der_r], writes=[mix["TX"]])
            rec.op("act", lambda e: e.activation(out=TX, in_=TX, func=AF.Ln, bias=pcol(C_ONE), scale=1.0),
                   reads=[mix["TX"], prm_r], writes=[mix["TX"]])
            rec.op("dve", lambda e: e.scalar_tensor_tensor(out=BB, in0=A2, scalar=0.5, in1=TX, op0=ALU.mult, op1=ALU.subtract),
                   reads=[mix["A2"], mix["TX"]], writes=[mix["BB"]])
            rec.op("act", lambda e: e.activation(out=BB, in_=BB, func=AF.Exp),
                   reads=[mix["BB"]], writes=[mix["BB"]])
            rec.op("dve", lambda e: e.tensor_tensor(out=BB, in0=BB, in1=XC, op=ALU.mult),
                   reads=[mix["BB"], mix[nXC]], writes=[mix["BB"]])
            rec.op("dve", lambda e: e.tensor_tensor_scan(out=HS, data0=TA, data1=BB, initial=STATE[:, h:h + 1],
                                                         op0=ALU.mult, op1=ALU.add),
                   reads=[mix["TA"], mix["BB"], state_r[h]], writes=[mix["HS"]])
            rec.op("pool", lambda e: e.tensor_copy(out=STATE[:, h:h + 1], in_=HS[:, T - 1:T]),
                   reads=[mix["HS"]], writes=[state_r[h]])
            if tapcol is not None:
                tap1("a", h * 128, TA, mix["TA"], tapcol)
                tap1("b", h * 128, BB, mix["BB"], tapcol)
                tap1("hs", h * 128, HS, mix["HS"], tapcol)

        def rnn_C(h, statA, mix=v3.mix):
            bank = next_bank()
            win_panel_group(16 + h, RA, bank)
            P = PS[bank][:]
            rec.op("act", lambda e: e.activation(out=GQ, in_=P, func=AF.Square), reads=[ps_res[bank]], writes=[mix["GQ"]])
            rec.op("dve", lambda e: e.tensor_scalar(out=GQ, in0=GQ, scalar1=0.044715, scalar2=1.0, op0=ALU.mult, op1=ALU.add),
                   reads=[mix["GQ"]], writes=[mix["GQ"]])
            rec.op("dve", lambda e: e.tensor_tensor(out=GQ, in0=GQ, in1=P, op=ALU.mult),
                   reads=[mix["GQ"], ps_res[bank]], writes=[mix["GQ"]])
            rec.op("act", lambda e: e.activation(out=GQ, in_=GQ, func=AF.Exp, scale=-2.0 * 0.7978845608028654),
                   reads=[mix["GQ"]], writes=[mix["GQ"]])
            rec.op("act", lambda e: e.activation(out=GQ, in_=GQ, func=AF.Ln, bias=pcol(C_ONE), scale=1.0),
                   reads=[mix["GQ"], prm_r], writes=[mix["GQ"]])
            rec.op("act", lambda e: e.activation(out=GQ, in_=GQ, func=AF.Exp, scale=-1.0),
                   reads=[mix["GQ"]], writes=[mix["GQ"]])
            rec.op("dve", lambda e: e.tensor_tensor(out=GQ, in0=GQ, in1=P, op=ALU.mult),
                   reads=[mix["GQ"], ps_res[bank]], writes=[mix["GQ"]])
            return bank

        def rnn_C2(h, statA, mix=v3.mix):
            rec.op("dve", lambda e: e.tensor_tensor(out=R1c(h), in0=GQ, in1=HS, op=ALU.mult),
                   reads=[mix["GQ"], mix["HS"]], writes=[r1[h]])
            statA.square(R1c(h), [r1[h]])

        def pool_V(c, first_tile, mix=v3.mix):
            g = c // 4
            bank = next_bank()
            win_panel_group(32 + c, RA, bank)
            rec.op("pool", lambda e: e.tensor_copy(out=V[:, 0:16], in_=VHALO[:, c * 16:(c + 1) * 16]),
                   reads=[vhalo_r[c]], writes=[mix["V"]])
            rec.op("act", lambda e: e.activation(out=V[:, 16:528], in_=PS[bank][:], func=AF.Copy),
                   reads=[ps_res[bank]], writes=[mix["V"]])
            rec.op("pool", lambda e: e.tensor_copy(out=VHALO[:, c * 16:(c + 1) * 16], in_=V[:, 512:528]),
                   reads=[mix["V"]], writes=[vhalo_r[c]])
            srcs = [(V, "V"), (SA, "SA"), (SBt, "SB"), (SA, "SA"), (SBt, "SB")]
            cur, curn = V, "V"
            sh = 1
            for lvl in range(g + 1):
                dst, dstn = srcs[lvl + 1]
                lo = 2 * sh - 1
                rec.op("dve", lambda e, dst=dst, cur=cur, lo=lo, sh=sh: e.tensor_tensor(
                    out=dst[:, lo:528], in0=cur[:, lo:528], in1=cur[:, lo - sh:528 - sh], op=ALU.add),
                    reads=[mix[curn]], writes=[mix[dstn]])
                cur, curn = dst, dstn
                sh *= 2
            w = 2 ** (g + 1)
            zi = c % 4
            rec.op("dve", lambda e, cur=cur: e.scalar_tensor_tensor(out=ZGr[zi], in0=cur[:, 16:528], scalar=1.0 / w,
                                                                     in1=V[:, 16:528], op0=ALU.mult, op1=ALU.subtract),
                   reads=[mix[curn], mix["V"]], writes=[mix[f"ZG{zi}"]])
            if first_tile:
                rec.op("dve", lambda e, cur=cur: e.tensor_tensor(out=TMP16[:], in0=cur[:, 16:32],
                                                                  in1=PRM[:, C_INVCNT + g * 16:C_INVCNT + (g + 1) * 16],
                                                                  op=ALU.mult),
                       reads=[mix[curn], prm_r], writes=[tmp16_r])
                rec.op("dve", lambda e: e.tensor_tensor(out=ZGr[zi][:, 0:16], in0=TMP16[:], in1=V[:, 16:32],
                                                        op=ALU.subtract),
                       reads=[tmp16_r, mix["V"]], writes=[mix[f"ZG{zi}"]])

        def pool_LIN(g, statB, mix=v3.mix):
            wt, wr = wload(wpool_d[g * 128:(g + 1) * 128, :], 2048)
            for jo in range(4):
                bank = next_bank()
                pairs = [(wt[:, ki * 512 + jo * 128: ki * 512 + (jo + 1) * 128], ZGr[ki]) for ki in range(4)]
                mm_group(bank, pairs, [wr] + [mix[f"ZG{ki}"] for ki in range(4)])
                c = g * 4 + jo
                rec.op("act", lambda e, bank=bank, c=c: e.activation(out=R1c(16 + c), in_=PS[bank][:], func=AF.Identity,
                                                                      bias=pcol(C_BPOOL + c)),
                       reads=[ps_res[bank], prm_r], writes=[r1[16 + c]])
                statB.square(R1c(16 + c), [r1[16 + c]])

        def light_pass():
            tiles = [(slot, i) for slot in range(NPRIOR) for i in range(NT)]
            L = len(tiles)
            regs = [RA, RB]

            def src(n):
                return (xp_d, tiles[n][0] * D, tiles[n][1] * T) if n < L else (xm_d, 0, 0)
            r3_switch("mix")
            reg0 = regs[L % 2]
            load_x(reg0, *src(0))
            NormJob(reg0, C_GMIX).all()
            for n in range(L):
                slot, i = tiles[n]
                reg, nxt = regs[(L - n) % 2], regs[(L - n - 1) % 2]
                job = NormJob(nxt, C_GMIX)
                for h in range(16):
                    rnn_A(h, alt=h % 3, reg=reg)
                    if h >= 2:
                        rnn_B(h - 2, alt=(h - 2) % 3)
                    if h < 8:
                        load_x(nxt, *src(n + 1), queue="act", ks=range(4 * h, 4 * h + 4))
                    if 2 <= h <= 9:
                        job.squares(4)
                    if h == 10:
                        job.finish()
                    if h >= 10:
                        job.scales(6 if h < 12 else 5)
                rnn_B(14, alt=14 % 3)
                rnn_B(15, alt=15 % 3)
                if slot == NPRIOR - 1 and i == NT - 1:
                    for c in range(16):
                        bank = next_bank()
                        win_panel_group(32 + c, reg, bank, ncol=16, rhs=lambda k, reg=reg: reg.r(k)[:, T - 16:T])
                        rec.op("dve", lambda e, bank=bank, c=c: e.tensor_scalar(
                            out=VHALO[:, c * 16:(c + 1) * 16], in0=PS[bank][:, 0:16], scalar1=pcol(C_MASK + slot),
                            scalar2=None, op0=ALU.mult), reads=[ps_res[bank], prm_r], writes=[vhalo_r[c]])
                if i == NT - 1:
                    rec.op("dve", lambda e, slot=slot: e.tensor_scalar(out=STATE[:], in0=STATE[:], scalar1=pcol(C_MASK + slot),
                                                                       scalar2=None, op0=ALU.mult),
                           reads=state_r + [prm_r], writes=state_r)
                    rec.op("dve", lambda e, slot=slot: e.tensor_scalar(out=HALO[:], in0=HALO[:], scalar1=pcol(C_MASK + slot),
                                                                       scalar2=None, op0=ALU.mult),
                           reads=halo_r + [prm_r], writes=halo_r)

        def main_tile(i, prefetched=False):
            col0 = i * T
            r3_switch("mix")
            if not prefetched:
                load_x(RA, xm_d, 0, col0)
                NormJob(RA, C_GMIX).all()
            tap("u1", R2c, r2, col0)
            statA = Stat(16)
            for h in range(16):
                rnn_A(h)
                rnn_C(h, statA)
                statA.flush()
                rnn_B(h, tapcol=col0)
                rnn_C2(h, statA)
            statB = Stat(16)
            for c in range(16):
                pool_V(c, first_tile=(i == 0))
                if c % 4 == 3:
                    pool_LIN(c // 4, statB)
            for k in range(NK):
                rec.dma(R2c(k), xm_d[k * 128:(k + 1) * 128, col0:col0 + T], reads=[], writes=[r2[k]], sem=r2_sem[k])
            statA.finish(DR)
            for h in range(16):
                scale_to(R1r(h), r1[h], R1c(h), r1[h], C_BETA + h)
            statB.finish(DP)
            for c in range(16):
                scale_to(R1r(16 + c), r1[16 + c], R1c(16 + c), r1[16 + c], C_PSCALE + c)

            tap("mix", R1c, r1, col0)
            def big_proj(wd, m, rhs, rhs_res):
                bank = next_bank()
                proj_group(wd, m, rhs, rhs_res, bank)
                return bank

            st2 = Stat(NK)
            for m in range(NK):
                bank = big_proj(wout_d, m, R1r, r1)
                rec.op("dve", lambda e, bank=bank, m=m: e.tensor_tensor(out=R2c(m), in0=PS[bank][:], in1=R2c(m), op=ALU.add),
                       reads=[ps_res[bank], r2[m]], writes=[r2[m]])
                st2.square(R2c(m), [r2[m]])
            st2.finish(D)
            tap("h1", R2c, r2, col0)
            for k in range(NK):
                scale_to(R1r(k), r1[k], R2c(k), r2[k], C_GMLP + k)

            r3_switch("hb")

            def mlp_up(grp):
                b = grp % 2
                for j in range(8):
                    f = grp * 8 + j
                    bank = big_proj(wup_d, f, R1r, r1)
                    rec.op("act", lambda e, bank=bank, b=b, j=j: e.activation(out=HB[b][j], in_=PS[bank][:], func=AF.Relu),
                           reads=[ps_res[bank]], writes=[v3.hb[b][j]])
                    rec.op("dve", lambda e, b=b, j=j: e.tensor_tensor(out=HBr[b][j], in0=HB[b][j], in1=HB[b][j], op=ALU.mult),
                           reads=[v3.hb[b][j]], writes=[v3.hb[b][j]])

            st3 = Stat(NK)

            def mlp_down(grp):
                b = grp % 2
                last = grp == NGRP - 1
                for mp in range(16):
                    row = (grp * 16 + mp) * 128
                    wt, wr = wload(wdn_d[row:row + 128, :], 2048)
                    for mm_ in range(2):
                        m = mp * 2 + mm_
                        bank = next_bank()
                        pairs = [(wt[:, (mm_ * 8 + kc) * 128:(mm_ * 8 + kc + 1) * 128], HBr[b][kc]) for kc in range(8)]
                        mm_group(bank, pairs, [wr] + v3.hb[b])
                        rec.op("dve", lambda e, bank=bank, m=m: e.tensor_tensor(out=R2c(m), in0=PS[bank][:], in1=R2c(m), op=ALU.add),
                               reads=[ps_res[bank], r2[m]], writes=[r2[m]])
                        if last:
                            st3.square(R2c(m), [r2[m]])

            mlp_up(0)
            for grp in range(NGRP):
                if grp + 1 < NGRP:
                    mlp_up(grp + 1)
                mlp_down(grp)
            st3.finish(D)
            tap("h2", R2c, r2, col0)
            for k in range(NK):
                scale_to(R1r(k), r1[k], R2c(k), r2[k], C_GPLE + k)

            r3_switch("ple")
            for kp in range(2):
                rec.dma(PTr[kp], pt_d[kp * 128:(kp + 1) * 128, col0:col0 + T].bitcast(F32R), reads=[],
                        writes=[v3.ple["PT"]], sem=pt_sem, cast=True)
            st4 = Stat(NK)
            for m in range(NK):
                bg = big_proj(wgate_d, m, R1r, r1)
                wt, wr = wload_small(wproj_d[m * 128:(m + 1) * 128, :])
                bp = next_bank()
                mm_group(bp, [(wt[:, kp * 128:(kp + 1) * 128], PTr[kp]) for kp in range(2)], [wr, v3.ple["PT"]])
                rec.op("act", lambda e, bg=bg: e.activation(out=TG, in_=PS[bg][:], func=AF.Exp, scale=-1.0),
                       reads=[ps_res[bg]], writes=[v3.ple["TG"]])
                rec.op("act", lambda e: e.activation(out=TG, in_=TG, func=AF.Ln, bias=pcol(C_ONE), scale=1.0),
                       reads=[v3.ple["TG"], prm_r], writes=[v3.ple["TG"]])
                rec.op("act", lambda e: e.activation(out=TG, in_=TG, func=AF.Exp, scale=-1.0),
                       reads=[v3.ple["TG"]], writes=[v3.ple["TG"]])
                rec.op("dve", lambda e, bp=bp: e.tensor_tensor(out=TG2, in0=TG, in1=PS[bp][:], op=ALU.mult),
                       reads=[v3.ple["TG"], ps_res[bp]], writes=[v3.ple["TG2"]])
                rec.op("dve", lambda e, m=m: e.tensor_tensor(out=R2c(m), in0=TG2, in1=R2c(m), op=ALU.add),
                       reads=[v3.ple["TG2"], r2[m]], writes=[r2[m]])
                st4.square(R2c(m), [r2[m]])
            st4.finish(D)
            tap("h3", R2c, r2, col0)
            for m in range(NK):
                scale_to(R2c(m), r2[m], R2c(m), r2[m], C_GFIN + m)
                rec.dma(out_d[m * 128:(m + 1) * 128, col0:col0 + T], R2c(m), reads=[r2[m]], writes=[], sem=r2_sem[m])

        light_pass()
        for i in range(NT):
            main_tile(i, prefetched=(i == 0))
        rec.final_wait("sp", r2_sem + r1_sem + [tap_sem])

        with nc.Block() as block:
            @block.tensor
            def _(e):
                rec.replay(nc, e, "pe")

            @block.scalar
            def _(e):
                rec.replay(nc, e, "act")

            @block.vector
            def _(e):
                rec.replay(nc, e, "dve")

            @block.gpsimd
            def _(e):
                rec.replay(nc, e, "pool")

            @block.sync
            def _(e):
                rec.replay(nc, e, "sp")
    return nc


def _chunked(v, n):
    return np.ascontiguousarray(np.asarray(v, np.float32).reshape(n, 128).T)


def _panel_k2(w):
    K, M = w.shape
    assert K == 4096
    a = w.reshape(2, 16, 128, M // 128, 128)
    a = a.transpose(3, 0, 2, 1, 4)
    return np.ascontiguousarray(a).reshape(-1, 2048)


def prep_weights(w_in, w_rg_a, w_rg_x, w_pool, w_out, w_up, w_down, w_ple_gate, w_ple_proj):
    DFF = w_up.shape[1]
    ngrp = DFF // 1024
    d = {}
    d["win"] = _panel_k2(w_in)
    wg = np.concatenate([w_rg_a, w_rg_x], axis=2)
    d["wg"] = np.ascontiguousarray(wg).reshape(16 * 128, 256)
    wp = w_pool.reshape(4, 4, 128, 512).transpose(0, 2, 1, 3)
    d["wpool"] = np.ascontiguousarray(wp).reshape(4 * 128, 2048)
    d["wout"] = _panel_k2(w_out)
    d["wup"] = _panel_k2(w_up)
    a = w_down.reshape(ngrp, 8, 128, 16, 2, 128)
    a = a.transpose(0, 3, 2, 4, 1, 5)
    d["wdn"] = np.ascontiguousarray(a).reshape(ngrp * 16 * 128, 2048)
    d["wgate"] = _panel_k2(w_ple_gate)
    a = w_ple_proj.reshape(2, 128, 32, 128).transpose(2, 1, 0, 3)
    d["wproj"] = np.ascontiguousarray(a).reshape(32 * 128, 256)
    return d


def prep_prm(q, norm_mix_g, norm_mlp_g, norm_ple_g, final_norm_g, conv_w, conv_b, b_rg_a, b_rg_x, lru_lambda,
             beta_rnn, b_pool, pool_scale, nprior=3):
    prm = np.zeros((128, NPRM), np.float32)
    prm[:, C_GMIX:C_GMIX + 32] = _chunked(norm_mix_g, 32)
    prm[:, C_GMLP:C_GMLP + 32] = _chunked(norm_mlp_g, 32)
    prm[:, C_GPLE:C_GPLE + 32] = _chunked(norm_ple_g, 32)
    prm[:, C_GFIN:C_GFIN + 32] = _chunked(final_norm_g, 32)
    for k in range(4):
        prm[:, C_CONVW + k * 16:C_CONVW + (k + 1) * 16] = _chunked(conv_w[k], 16)
    prm[:, C_CONVB:C_CONVB + 16] = _chunked(conv_b, 16)
    prm[:, C_BA:C_BA + 16] = _chunked(b_rg_a, 16)
    prm[:, C_BX:C_BX + 16] = _chunked(b_rg_x, 16)
    prm[:, C_LAM:C_LAM + 16] = _chunked(lru_lambda, 16)
    prm[:, C_BETA:C_BETA + 16] = _chunked(beta_rnn, 16)
    prm[:, C_BPOOL:C_BPOOL + 16] = _chunked(b_pool, 16)
    prm[:, C_PSCALE:C_PSCALE + 16] = _chunked(pool_scale, 16)
    prm[:, C_EPS] = EPS
    prm[:, C_ONE] = 1.0
    prm[:, C_NEG] = -1e-7
    for j in range(nprior):
        prm[:, C_MASK + j] = 1.0 if (q - nprior + j) >= 0 else 0.0
    for g in range(4):
        w = 2 ** (g + 1)
        for t in range(16):
            cnt = min(t + 1, w) if q == 0 else w
            prm[:, C_INVCNT + g * 16 + t] = np.float32(1.0) / np.float32(cnt)
    return prm


def run(inputs, NT, trace=False, debug_taps=False):
    x = np.asarray(inputs["x"], np.float32)
    p = np.asarray(inputs["p"], np.float32)[0]
    B, S, _ = x.shape
    QPB = NCORE // B
    SEG = S // QPB
    assert SEG == NT * T
    NPRIOR = QPB - 1
    DFF = inputs["w_up"].shape[2]
    wd = prep_weights(inputs["w_in"][0], inputs["w_rg_a"][0], inputs["w_rg_x"][0], inputs["w_pool"][0],
                      inputs["w_out"][0], inputs["w_up"][0], inputs["w_down"][0], inputs["w_ple_gate"][0],
                      inputs["w_ple_proj"][0])
    wd = {k: np.asarray(v, np.float32) for k, v in wd.items()}
    nc = build_nc(NT, DFF, NPRIOR, debug_taps=debug_taps)
    in_maps = []
    xT = [np.ascontiguousarray(x[b].T) for b in range(B)]
    pT = [np.ascontiguousarray(p[b].T) for b in range(B)]
    for c in range(NCORE):
        b, q = divmod(c, QPB)
        xm = np.ascontiguousarray(xT[b][:, q * SEG:(q + 1) * SEG])
        xp = np.zeros((NPRIOR * D, SEG), np.float32)
        for j in range(NPRIOR):
            sidx = q - NPRIOR + j
            if sidx >= 0:
                xp[j * D:(j + 1) * D] = xT[b][:, sidx * SEG:(sidx + 1) * SEG]
        prm = prep_prm(q, inputs["norm_mix_g"][0], inputs["norm_mlp_g"][0], inputs["norm_ple_g"][0],
                       inputs["final_norm_g"], inputs["conv_w"][0], inputs["conv_b"][0], inputs["b_rg_a"][0],
                       inputs["b_rg_x"][0], inputs["lru_lambda"][0], inputs["beta_rnn"][0], inputs["b_pool"][0],
                       inputs["pool_scale"][0], nprior=NPRIOR)
        m = {"xm": xm, "xp": xp, "pt": np.ascontiguousarray(pT[b][:, q * SEG:(q + 1) * SEG]), "prm": prm}
        m.update(wd)
        in_maps.append(m)
    res = run_bass_kernel_spmd(nc, in_maps, core_ids=list(range(NCORE)), trace=trace)
    out = np.empty((B, S, D), np.float32)
    for c in range(NCORE):
        b, q = divmod(c, QPB)
        out[b, q * SEG:(q + 1) * SEG, :] = res.results[c]["out"].T
    if debug_taps:
        taps = {}
        for n in ("u1", "mix", "h1", "h2", "h3", "xc", "a", "b", "hs"):
            a = np.empty((B, S, D), np.float32)
            for c in range(NCORE):
                b, q = divmod(c, QPB)
                a[b, q * SEG:(q + 1) * SEG, :] = res.results[c]["tap_" + n].T
            taps[n] = a
        return out, res, taps
    return out, res


def kernel(**inputs):
    out, _ = run(inputs, NT=4)
    return out
```
